# Optimizing a Trainium2 kernel written in Bass

```python
import jax, jax.numpy as jnp
from jax import lax
import numpy as np

D_MODEL = 1024
BATCH = 16
SEQ = 2048
DEPTH = 2

CHUNK = 64
RET_HEADS = 4
RET_QK_DIM = 128
RET_V_DIM = 256
RET_QK = RET_HEADS * RET_QK_DIM
RET_V = RET_HEADS * RET_V_DIM
SC_WIDTH = D_MODEL
SC_KERNEL = 3
CF_WIDTH = D_MODEL
CF_KERNEL = 31
N_BRANCH = 3
D_FF = 4 * D_MODEL
ROPE_BASE = 10000.0
NORM_EPS = 1e-6
LN_EPS = 1e-5
N_NORMS = 6
IN_SPLITS = (RET_QK, RET_QK, RET_V, RET_V, SC_WIDTH, SC_WIDTH, SC_WIDTH, 2 * CF_WIDTH, N_BRANCH * D_MODEL)
D_IN = 2 * RET_QK + 2 * RET_V + 3 * SC_WIDTH + 2 * CF_WIDTH + N_BRANCH * D_MODEL

kernel_name = "hybrid_retention_conv_macaron_encoder"


def rms_norm(x, g):
    xf = x.astype(jnp.float32)
    y = xf * lax.rsqrt(jnp.mean(xf * xf, axis=-1, keepdims=True) + NORM_EPS)
    return (y * g.astype(jnp.float32)).astype(x.dtype)


def layer_norm(x, g, b):
    xf = x.astype(jnp.float32)
    mu = jnp.mean(xf, axis=-1, keepdims=True)
    var = jnp.mean(jnp.square(xf - mu), axis=-1, keepdims=True)
    y = (xf - mu) * lax.rsqrt(var + LN_EPS)
    return (y * g.astype(jnp.float32) + b.astype(jnp.float32)).astype(x.dtype)


def swiglu_ffn(h, w_gu, w_down):
    gate, up = jnp.split(h @ w_gu, 2, axis=-1)
    return (jax.nn.silu(gate) * up) @ w_down


def causal_depthwise_conv(x, w):
    k = w.shape[0]
    return lax.conv_general_dilated(
        x, w[:, None, :].astype(x.dtype), window_strides=(1,), padding=[(k - 1, 0)],
        dimension_numbers=("NWC", "WIO", "NWC"), feature_group_count=x.shape[-1])


def rotary(x, positions):
    half = x.shape[-1] // 2
    inv_freq = ROPE_BASE ** (-jnp.arange(half, dtype=jnp.float32) / half)
    ang = positions.astype(jnp.float32)[..., None] * inv_freq
    cos = jnp.cos(ang)[:, :, None, :]
    sin = jnp.sin(ang)[:, :, None, :]
    x1, x2 = x[..., :half], x[..., half:]
    return jnp.concatenate([x1 * cos - x2 * sin, x2 * cos + x1 * sin], axis=-1)


def chunkwise_retention(q, k, v, positions):
    b, s, h, dk = q.shape
    dv = v.shape[-1]
    n = s // CHUNK
    q = rotary(q, positions)
    k = rotary(k, positions) * (dk ** -0.5)
    log_g = jnp.log(1.0 - 2.0 ** (-5.0 - jnp.arange(h, dtype=jnp.float32)))
    idx = jnp.arange(CHUNK, dtype=jnp.float32)
    decay_intra = jnp.exp(log_g[:, None, None] * jnp.abs(idx[:, None] - idx[None, :]))
    xi = jnp.exp(log_g[None, :] * (idx[:, None] + 1.0))
    zeta = jnp.exp(log_g[None, :] * (CHUNK - 1.0 - idx[:, None]))
    g_chunk = jnp.exp(log_g * CHUNK)

    qc = q.reshape(b, n, CHUNK, h, dk)
    kc = k.reshape(b, n, CHUNK, h, dk)
    vc = v.reshape(b, n, CHUNK, h, dv)
    scores = jnp.einsum("bnihd,bnjhd->bnhij", qc, kc) * decay_intra
    intra = jnp.einsum("bnhij,bnjhe->bnihe", scores, vc)

    kz = kc * zeta[None, None, :, :, None]

    def step(state, inp):
        q_n, k_n, v_n = inp
        inter = jnp.einsum("bihd,bhde->bihe", q_n, state) * xi[None, :, :, None]
        state = state * g_chunk[None, :, None, None] + jnp.einsum("bjhd,bjhe->bhde", k_n, v_n)
        return state, inter

    state0 = jnp.zeros((b, h, dk, dv), jnp.float32)
    _, inter = lax.scan(step, state0, (qc.swapaxes(0, 1), kz.swapaxes(0, 1), vc.swapaxes(0, 1)))
    out = intra + inter.swapaxes(0, 1)
    return out.reshape(b, s, h, dv)


def head_norm(o):
    mu = jnp.mean(o, axis=-1, keepdims=True)
    var = jnp.mean(jnp.square(o - mu), axis=-1, keepdims=True)
    return (o - mu) * lax.rsqrt(var + LN_EPS)


def hybrid_mixer(h, positions, w_in, w_ret_o, sc_conv_w, w_sc_o,
                 cf_dw_w, cf_dw_b, cf_ln_g, cf_ln_b, w_cf_o, w_o):
    b, s, _ = h.shape
    split_at = [int(v) for v in np.cumsum(IN_SPLITS)[:-1]]
    q, k, v, g_ret, sc_b, sc_c, sc_x, cf_in, gate_logits = jnp.split(h @ w_in, split_at, axis=-1)

    o = chunkwise_retention(
        q.reshape(b, s, RET_HEADS, RET_QK_DIM).astype(jnp.float32),
        k.reshape(b, s, RET_HEADS, RET_QK_DIM).astype(jnp.float32),
        v.reshape(b, s, RET_HEADS, RET_V_DIM).astype(jnp.float32),
        positions)
    o = head_norm(o).reshape(b, s, RET_V).astype(h.dtype)
    y_a = (jax.nn.silu(g_ret) * o) @ w_ret_o

    y_b = (sc_b * causal_depthwise_conv(sc_c * sc_x, sc_conv_w)) @ w_sc_o

    glu_a, glu_b = jnp.split(cf_in, 2, axis=-1)
    u = glu_a * jax.nn.sigmoid(glu_b)
    u = causal_depthwise_conv(u, cf_dw_w) + cf_dw_b
    u = jax.nn.silu(layer_norm(u, cf_ln_g, cf_ln_b))
    y_c = u @ w_cf_o

    gates = jax.nn.sigmoid(gate_logits).reshape(b, s, N_BRANCH, D_MODEL)
    merged = gates[:, :, 0] * y_a + gates[:, :, 1] * y_b + gates[:, :, 2] * y_c
    return merged @ w_o


def setup_inputs(seed: int = 0) -> dict:
    key = jax.random.key(seed)
    ks = jax.random.split(key, 20)
    f32 = jnp.float32

    def dense(k, shape, fan_in):
        return jax.random.normal(k, shape, f32) * (fan_in ** -0.5)

    x = jax.random.normal(ks[0], (BATCH, SEQ, D_MODEL), f32)
    offsets = jax.random.randint(ks[1], (BATCH, 1), 0, 64, dtype=jnp.int32) * CHUNK
    positions = offsets + jnp.arange(SEQ, dtype=jnp.int32)[None, :]
    return {
        "x": x,
        "positions": positions,
        "norm_g": 1.0 + 0.02 * jax.random.normal(ks[2], (DEPTH, N_NORMS, D_MODEL), f32),
        "ffn1_w_gu": dense(ks[3], (DEPTH, D_MODEL, 2 * D_FF), D_MODEL),
        "ffn1_w_down": dense(ks[4], (DEPTH, D_FF, D_MODEL), D_FF),
        "w_in": dense(ks[5], (DEPTH, D_MODEL, D_IN), D_MODEL),
        "w_ret_o": dense(ks[6], (DEPTH, RET_V, D_MODEL), RET_V),
        "sc_conv_w": dense(ks[7], (DEPTH, SC_KERNEL, SC_WIDTH), SC_KERNEL),
        "w_sc_o": dense(ks[8], (DEPTH, SC_WIDTH, D_MODEL), SC_WIDTH),
        "cf_dw_w": dense(ks[9], (DEPTH, CF_KERNEL, CF_WIDTH), CF_KERNEL),
        "cf_dw_b": 0.02 * jax.random.normal(ks[10], (DEPTH, CF_WIDTH), f32),
        "cf_ln_g": 1.0 + 0.02 * jax.random.normal(ks[11], (DEPTH, CF_WIDTH), f32),
        "cf_ln_b": 0.02 * jax.random.normal(ks[12], (DEPTH, CF_WIDTH), f32),
        "w_cf_o": dense(ks[13], (DEPTH, CF_WIDTH, D_MODEL), CF_WIDTH),
        "w_o": dense(ks[14], (DEPTH, D_MODEL, D_MODEL), D_MODEL),
        "ffn2_w_gu": dense(ks[15], (DEPTH, D_MODEL, 2 * D_FF), D_MODEL),
        "ffn2_w_down": dense(ks[16], (DEPTH, D_FF, D_MODEL), D_FF),
    }


def reference(x, positions, norm_g, ffn1_w_gu, ffn1_w_down, w_in, w_ret_o, sc_conv_w, w_sc_o,
              cf_dw_w, cf_dw_b, cf_ln_g, cf_ln_b, w_cf_o, w_o, ffn2_w_gu, ffn2_w_down):
    for l in range(DEPTH):
        g = norm_g[l]
        x = x + 0.5 * rms_norm(swiglu_ffn(rms_norm(x, g[0]), ffn1_w_gu[l], ffn1_w_down[l]), g[1])
        m = hybrid_mixer(rms_norm(x, g[2]), positions, w_in[l], w_ret_o[l], sc_conv_w[l], w_sc_o[l],
                         cf_dw_w[l], cf_dw_b[l], cf_ln_g[l], cf_ln_b[l], w_cf_o[l], w_o[l])
        x = x + rms_norm(m, g[3])
        x = x + 0.5 * rms_norm(swiglu_ffn(rms_norm(x, g[4]), ffn2_w_gu[l], ffn2_w_down[l]), g[5])
    return x
```

```python
import math
import numpy as np
import concourse.bass as bass
import concourse.mybir as mybir
from concourse.bass_utils import run_bass_kernel_spmd

F32 = mybir.dt.float32
BF16 = mybir.dt.bfloat16
I32 = mybir.dt.int32
AF = mybir.ActivationFunctionType
ALU = mybir.AluOpType

S = 2048
D = 1024
C = 8
T = 4
TT = 512
NSEQ = 2
L = 2
DFF = 4096
DIN = 11264
NCORES = 8
DK = 128
O_Q, O_K, O_V, O_GR, O_SCB, O_SCC, O_SCX, O_GA, O_GB, O_GATE = 0, 512, 1024, 2048, 3072, 4096, 5120, 6144, 7168, 8192
BIGD = 1.0e5

P_NG = 0
P_SCW = P_NG + L * 6 * C
P_CFW = P_SCW + L * 3 * C
P_CFB = P_CFW + L * 31 * C
P_LNG = P_CFB + L * C
P_LNB = P_LNG + L * C
P_INVF = P_LNB + L * C
P_SGN = P_INVF + 1
P_ID = P_SGN + 1
NPAR = P_ID + 128

WNAMES = ["ffn1_w_gu", "ffn1_w_down", "w_in", "w_ret_o", "w_sc_o", "w_cf_o", "w_o", "ffn2_w_gu", "ffn2_w_down"]
WSHAPES = {"ffn1_w_gu": [L, D, 2 * DFF], "ffn1_w_down": [L, DFF, D], "w_in": [L, D, DIN],
           "w_ret_o": [L, D, D], "w_sc_o": [L, D, D], "w_cf_o": [L, D, D], "w_o": [L, D, D],
           "ffn2_w_gu": [L, D, 2 * DFF], "ffn2_w_down": [L, DFF, D]}


class Buf:
    __slots__ = ("w", "r", "dsem", "dcnt")

    def __init__(self):
        self.w = None
        self.r = {}
        self.dsem = None
        self.dcnt = 0


class TB:
    __slots__ = ("ap", "b")

    def __init__(self, ap, b=None):
        self.ap = ap
        self.b = b if b is not None else Buf()


class XSt:
    __slots__ = ("ap", "bs")

    def __init__(self, ap):
        self.ap = ap
        self.bs = [Buf() for _ in range(C)]


class Prog:
    def __init__(self, nc):
        self.nc = nc
        self.engs = {"pe": nc.tensor, "dve": nc.vector, "act": nc.scalar, "pool": nc.gpsimd, "sp": nc.sync}
        self.sem = {k: nc.alloc_semaphore(f"s_{k}") for k in self.engs}
        self.cnt = {k: 0 for k in self.engs}
        self.waited = {k: {} for k in self.engs}
        self.nsem = 0
        self.dma_toks = []

    def _wait(self, e, tok):
        sem, val = tok
        if e == "pe" and sem is self.sem["pe"]:
            return
        key = id(sem)
        if self.waited[e].get(key, 0) >= val:
            return
        self.engs[e].wait_ge(sem, val)
        self.waited[e][key] = val

    def deps(self, e, reads, writes):
        for b in reads:
            if b.w is not None:
                self._wait(e, b.w)
        for b in writes:
            if b.w is not None:
                self._wait(e, b.w)
            for t in b.r.values():
                self._wait(e, t)

    def commit(self, tok, reads, writes):
        k = id(tok[0])
        for b in reads:
            o = b.r.get(k)
            if o is None or o[1] < tok[1]:
                b.r[k] = tok
        for b in writes:
            b.w = tok
            b.r = {}

    def op(self, e, fn, reads=(), writes=()):
        self.deps(e, reads, writes)
        ins = fn(self.engs[e])
        self.cnt[e] += 1
        ins.then_inc(self.sem[e], 1)
        tok = (self.sem[e], self.cnt[e])
        self.commit(tok, reads, writes)
        return tok

    def mm_group(self, fns, reads=(), writes=()):
        e = "pe"
        self.deps(e, reads, writes)
        ins = None
        for fn in fns:
            ins = fn(self.engs[e])
        self.cnt[e] += 1
        ins.then_inc(self.sem[e], 1)
        tok = (self.sem[e], self.cnt[e])
        self.commit(tok, reads, writes)
        return tok

    def dma(self, q, out, in_, reads=(), writes=()):
        self.deps(q, reads, writes)
        tgt = writes[0] if writes else reads[0]
        if tgt.dsem is None:
            tgt.dsem = self.nc.alloc_semaphore(f"d{self.nsem}")
            self.nsem += 1
        tgt.dcnt += 16
        self.engs[q].dma_start(out=out, in_=in_).then_inc(tgt.dsem, 16)
        tok = (tgt.dsem, tgt.dcnt)
        self.commit(tok, reads, writes)
        self.dma_toks.append(tok)
        return tok

    def retire(self, bufs):
        toks = []
        for b in bufs:
            if b.w is not None:
                toks.append(b.w)
            toks += list(b.r.values())
        for e in self.engs:
            for t in toks:
                self._wait(e, t)

    def barrier(self):
        toks = [(self.sem[k], self.cnt[k]) for k in self.engs if self.cnt[k] > 0] + self.dma_toks
        for e in self.engs:
            for t in toks:
                self._wait(e, t)
        self.dma_toks = []


def build(nl=L, nseq=NSEQ, stop=None):
    nc = bass.Bass("TRN2", target_bir_lowering=False)
    P = Prog(nc)

    def din(name, shape, dt=F32):
        return nc.dram_tensor(name, shape, dt, kind="ExternalInput").ap()

    x_d = din("x", [NSEQ, S, D])
    pos_d = din("pos", [NSEQ, S], I32)
    par_d = din("par", [128, NPAR])
    wtab_d = din("wtab", [128, 1408])
    wd = {n: din(n, WSHAPES[n]) for n in WNAMES}
    out_d = nc.dram_tensor("out", [NSEQ, S, D], F32, kind="ExternalOutput").ap()
    xs_d = nc.dram_tensor("xs", [NSEQ, C, 128, S], F32, kind="Internal").ap()
    xsb = [[Buf() for _ in range(T)] for _ in range(NSEQ)]
    outb = [[Buf() for _ in range(T)] for _ in range(NSEQ)]

    def sb(name, shape, dt):
        return nc.alloc_sbuf_tensor(name, shape, dt).ap()

    H = sb("H", [128, C, S], BF16)
    Hb = [[Buf() for _ in range(T)] for _ in range(C)]
    RA = sb("RA", [128, 16384], F32)
    ACC = RA.rearrange("p (c n) -> p c n", c=C)
    RAb16 = RA.bitcast(BF16)
    M = RAb16[:, 0:16384].rearrange("p (c n) -> p c n", c=C)
    U = RAb16[:, 16384:32768].rearrange("p (c n) -> p c n", c=C)
    ACCb = [[Buf() for _ in range(T)] for _ in range(C)]
    Mb = [[ACCb[c // 2][(c % 2) * 2 + t // 2] for t in range(T)] for c in range(C)]
    Ub = [[ACCb[4 + c // 2][(c % 2) * 2 + t // 2] for t in range(T)] for c in range(C)]
    R2 = sb("R2", [128, 8192], F32)
    R2b16 = R2.bitcast(BF16)
    NSLOT = 6
    RING = sb("RING", [128, NSLOT, 2048], BF16)
    ringb = [Buf() for _ in range(NSLOT)]
    XST = XSt(sb("XST", [128, C, TT], F32))
    PAR = TB(sb("PAR", [128, NPAR], F32))
    WTAB = TB(sb("WTAB", [128, 1408], F32))
    DTAB = TB(sb("DTAB", [128, 1408], F32))
    IDB = TB(sb("IDB", [128, 128], BF16))
    ONESB = TB(sb("ONESB", [128, 128], BF16))
    SQ = [TB(sb(f"SQ{i}", [128, TT], BF16)) for i in range(2)]
    HNB = [TB(sb(f"HNB{i}", [128, TT], BF16)) for i in range(4)]
    ST = [TB(sb(f"ST{i}", [128, TT], F32)) for i in range(4)]
    AC = [TB(sb(f"AC{i}", [128, TT], F32)) for i in range(3)]
    ACTB_raw = [sb(f"ACTB{i}", [128, 2, TT], BF16) for i in range(2)]
    ACTBb = [[Buf(), Buf()] for _ in range(2)]
    PT = [TB(ACTB_raw[0][:, 0, :], ACTBb[0][0]), TB(ACTB_raw[0][:, 1, :], ACTBb[0][1]), TB(ACTB_raw[1][:, 0, :], ACTBb[1][0])]
    PS = [TB(nc.alloc_psum_tensor(f"ps{i}", [128, TT], F32).ap()) for i in range(8)]

    ctr = {"bank": 0, "sq": 0, "st": 0, "ac": 0, "pt": 0, "ring": 0}
    rot = {"banks": list(range(8))}

    def bank():
        lst = rot["banks"]
        i = lst[ctr["bank"] % len(lst)]
        ctr["bank"] += 1
        return PS[i]

    def rotbuf(lst, key):
        i = ctr[key] % len(lst)
        ctr[key] += 1
        return lst[i]

    def sqbuf():
        return rotbuf(SQ, "sq")

    def stbuf():
        return rotbuf(ST, "st")

    def acbuf():
        return rotbuf(AC, "ac")

    def ptbuf():
        return rotbuf(PT, "pt")

    def pcol(off):
        return PAR.ap[:, off:off + 1]

    def tsl(t):
        return slice(t * TT, (t + 1) * TT)

    IDENT = PAR.ap[:, P_ID:P_ID + 128]

    P.dma("sp", PAR.ap, par_d, writes=[PAR.b])
    P.dma("sp", WTAB.ap, wtab_d, writes=[WTAB.b])
    P.op("dve", lambda e: e.tensor_copy(out=IDB.ap, in_=IDENT), reads=[PAR.b], writes=[IDB.b])
    P.op("dve", lambda e: e.memset(ONESB.ap, 1.0), writes=[ONESB.b])

    def wslot():
        i = ctr["ring"] % NSLOT
        ctr["ring"] += 1
        return i

    def wcols(name, l, c0, n):
        return wd[name][l].rearrange("(k p) n -> p k n", p=128)[:, :, c0:c0 + n]

    def load_unit(pieces, a, bcols):
        i = wslot()
        view = RING[:, i, 0:a * bcols].rearrange("p (a b) -> p a b", a=a)
        for (c0, n, src) in pieces:
            P.dma("pool", view[:, :, c0:c0 + n], src, writes=[ringb[i]])
        return TB(view, ringb[i])

    def proj(w, col0, t, pb, src=H, srcb=Hb):
        P.mm_group([
            (lambda pe, k=k: pe.matmul(pb.ap, lhsT=w.ap[:, k, col0:col0 + 128], rhs=src[:, k, tsl(t)],
                                       start=(k == 0), stop=(k == C - 1)))
            for k in range(C)], reads=[w.b] + [srcb[k][t] for k in range(C)], writes=[pb.b])

    def rstd_from_ssq(pb, n, eps, mul=1.0):
        sd = stbuf()
        P.op("act", lambda e: e.activation(out=sd.ap, in_=pb.ap, func=AF.Ln, bias=eps, scale=1.0 / n),
             reads=[pb.b], writes=[sd.b])
        if mul == 1.0:
            P.op("act", lambda e: e.activation(out=sd.ap, in_=sd.ap, func=AF.Exp, scale=-0.5), reads=[sd.b], writes=[sd.b])
        else:
            P.op("act", lambda e: e.activation(out=sd.ap, in_=sd.ap, func=AF.Exp, scale=-0.5, bias=float(math.log(mul))), reads=[sd.b], writes=[sd.b])
        return sd

    def ssq_bank(srcs):
        pb = bank()
        n = len(srcs)
        for i, (ap, b) in enumerate(srcs):
            sq = sqbuf()
            P.op("act", lambda e, ap=ap, sq=sq: e.activation(out=sq.ap, in_=ap, func=AF.Square), reads=[b], writes=[sq.b])
            P.mm_group([lambda pe, sq=sq, i=i: pe.matmul(pb.ap, lhsT=ONESB.ap, rhs=sq.ap, start=(i == 0), stop=(i == n - 1))],
                       reads=[sq.b, ONESB.b], writes=[pb.b])
        return pb

    live = {"R2": [], "XST": []}

    def carve(new_bufs, region="R2"):
        if live[region] and set(map(id, live[region])) == set(map(id, new_bufs)):
            return
        toks = {}
        for b in live[region]:
            for tok in ([b.w] if b.w is not None else []) + list(b.r.values()):
                k = id(tok[0])
                if k not in toks or toks[k][1] < tok[1]:
                    toks[k] = tok
        for b in new_bufs:
            for k, tok in toks.items():
                if k not in b.r or b.r[k][1] < tok[1]:
                    b.r[k] = tok
        live[region] = list(new_bufs)

    def load_x(seq, t, xst=None):
        xst = xst or XST
        if xst is XST:
            carve(XST.bs, "XST")
        P.dma("sp", xst.ap, xs_d[seq, :, :, tsl(t)].rearrange("c p n -> p c n"), reads=[xsb[seq][t]], writes=xst.bs)

    def store_x(seq, t, xst=None):
        xst = xst or XST
        P.dma("sp", xs_d[seq, :, :, tsl(t)].rearrange("c p n -> p c n"), xst.ap, reads=xst.bs, writes=[xsb[seq][t]])

    def h_from_xst(l, ni, t, xst=None):
        xst = xst or XST
        pb = ssq_bank([(xst.ap[:, c, :], xst.bs[c]) for c in range(C)])
        rs = rstd_from_ssq(pb, D, 1e-6)
        for c in range(C):
            P.op("dve", lambda e, c=c: e.scalar_tensor_tensor(out=H[:, c, tsl(t)], in0=xst.ap[:, c, :],
                                                              scalar=pcol(P_NG + (l * 6 + ni) * C + c), in1=rs.ap,
                                                              op0=ALU.mult, op1=ALU.mult),
                 reads=[xst.bs[c], rs.b, PAR.b], writes=[Hb[c][t]])

    def prenorm(l, seq, ni):
        for t in range(T):
            load_x(seq, t)
            h_from_xst(l, ni, t)

    def postnorm_tile(l, seq, ni, t, rfn, alpha, nxt=None, xst=None):
        xst = xst or XST
        load_x(seq, t, xst)
        pb = ssq_bank([rfn(c) for c in range(C)])
        rs = rstd_from_ssq(pb, D, 1e-6, mul=alpha)
        for c in range(C):
            ap, b = rfn(c)
            tmp = acbuf()
            P.op("dve", lambda e, ap=ap, tmp=tmp, c=c: e.scalar_tensor_tensor(
                out=tmp.ap, in0=ap, scalar=pcol(P_NG + (l * 6 + ni) * C + c), in1=rs.ap, op0=ALU.mult, op1=ALU.mult),
                reads=[b, rs.b, PAR.b], writes=[tmp.b])
            P.op("dve", lambda e, tmp=tmp, c=c: e.tensor_tensor(out=xst.ap[:, c, :], in0=tmp.ap, in1=xst.ap[:, c, :], op=ALU.add),
                 reads=[tmp.b, xst.bs[c]], writes=[xst.bs[c]])
        store_x(seq, t, xst)
        if nxt is not None:
            h_from_xst(nxt[0], nxt[1], t, xst)

    def ffn(l, seq, wgu, wdn, npre, npost, do_pre, nxt, hooks=None):
        if do_pre:
            prenorm(l, seq, npre)
        NG = DFF // 256
        hooks = hooks or {}

        def gate_up(i, g, t, wg, wu):
            ai = i % 2
            for jj in range(2):
                pg = bank()
                proj(wg, jj * 128, t, pg)
                pu = bank()
                proj(wu, jj * 128, t, pu)
                sg = acbuf()
                P.op("act", lambda e, sg=sg, pg=pg: e.activation(out=sg.ap, in_=pg.ap, func=AF.Silu),
                     reads=[pg.b], writes=[sg.b])
                P.op("dve", lambda e, sg=sg, pu=pu, jj=jj: e.tensor_tensor(out=ACTB_raw[ai][:, jj, :], in0=pu.ap, in1=sg.ap, op=ALU.mult),
                     reads=[pu.b, sg.b], writes=[ACTBb[ai][jj]])

        def down(i, g, t, wdv):
            ai = i % 2
            for c in range(C):
                pd = bank()
                P.mm_group([
                    (lambda pe, jj=jj: pe.matmul(pd.ap, lhsT=wdv.ap[:, jj, c * 128:(c + 1) * 128], rhs=ACTB_raw[ai][:, jj, :],
                                                 start=(jj == 0), stop=(jj == 1)))
                    for jj in range(2)], reads=[wdv.b, ACTBb[ai][0], ACTBb[ai][1]], writes=[pd.b])
                if g == 0:
                    P.op("act", lambda e, pd=pd: e.activation(out=ACC[:, c, tsl(t)], in_=pd.ap, func=AF.Copy),
                         reads=[pd.b], writes=[ACCb[c][t]])
                else:
                    P.op("dve", lambda e, pd=pd: e.tensor_tensor(out=ACC[:, c, tsl(t)], in0=pd.ap, in1=ACC[:, c, tsl(t)], op=ALU.add),
                         reads=[pd.b, ACCb[c][t]], writes=[ACCb[c][t]])

        def load_group(g):
            j0 = 2 * g
            wg = load_unit([(0, 256, wcols(wgu, l, j0 * 128, 256))], C, 256)
            wu = load_unit([(0, 256, wcols(wgu, l, DFF + j0 * 128, 256))], C, 256)
            wdv = load_unit([(0, 1024, wd[wdn][l][j0 * 128:(j0 + 2) * 128, :].rearrange("(j p) n -> p j n", p=128))], 2, 1024)
            return wg, wu, wdv

        steps = [(g, t) for t in range(T) for g in (0, 1)]
        steps += [(g, t) for g in range(2, NG - 2) for t in range(T)]
        steps += [(g, t) for t in range(T) for g in (NG - 2, NG - 1)]
        wts = {0: load_group(0)}
        left = {g: T for g in range(NG)}
        gate_up(0, steps[0][0], steps[0][1], wts[0][0], wts[0][1])
        xs2 = [XSt(R2[:, 0:4096].rearrange("p (c n) -> p c n", c=C)), XSt(R2[:, 4096:8192].rearrange("p (c n) -> p c n", c=C))]
        for i, (g, t) in enumerate(steps):
            if i + 1 < len(steps):
                g2, t2 = steps[i + 1]
                if g2 not in wts:
                    wts[g2] = load_group(g2)
                gate_up(i + 1, g2, t2, wts[g2][0], wts[g2][1])
            down(i, g, t, wts[g][2])
            left[g] -= 1
            if left[g] == 0:
                wts.pop(g)
            if i in hooks:
                hooks[i]()
            if g == NG - 1:
                if t == 0:
                    carve(xs2[0].bs + xs2[1].bs)
                postnorm_tile(l, seq, npost, t, lambda c: (ACC[:, c, tsl(t)], ACCb[c][t]), 0.5, nxt, xs2[t % 2])

    TWO_PI = 2.0 * math.pi
    CW1 = 6.28125
    CW2 = TWO_PI - CW1

    XSTF = XST.ap.rearrange("p c n -> p (c n)")
    ROPE = {"COS": XSTF[:, 0:2048], "SINS": XSTF[:, 2048:4096], "COSb": [Buf() for _ in range(T)], "SINSb": [Buf() for _ in range(T)]}

    def rope_tile(seq, t):
        COS, SINS, COSb, SINSb = ROPE["COS"], ROPE["SINS"], ROPE["COSb"], ROPE["SINSb"]
        carve(COSb + SINSb, "XST")
        if True:
            pi_ = stbuf()
            a = stbuf()
            ki = stbuf()
            kf = stbuf()
            P.dma("sp", pi_.ap.bitcast(I32), pos_d[seq:seq + 1, tsl(t)].partition_broadcast(128), writes=[pi_.b])
            P.op("dve", lambda e: e.tensor_copy(out=a.ap, in_=pi_.ap.bitcast(I32)), reads=[pi_.b], writes=[a.b])
            P.op("dve", lambda e: e.tensor_scalar(out=a.ap, in0=a.ap, scalar1=pcol(P_INVF), scalar2=None, op0=ALU.mult),
                 reads=[a.b, PAR.b], writes=[a.b])
            for (shift, dst, dstb, signed) in ((math.pi / 2, COS, COSb, False), (0.0, SINS, SINSb, True)):
                r = acbuf()
                m = acbuf()
                P.op("dve", lambda e: e.tensor_scalar(out=ki.ap.bitcast(I32), in0=a.ap, scalar1=1.0 / TWO_PI, scalar2=shift / TWO_PI,
                                                      op0=ALU.mult, op1=ALU.add), reads=[a.b], writes=[ki.b])
                P.op("dve", lambda e: e.tensor_copy(out=kf.ap, in_=ki.ap.bitcast(I32)), reads=[ki.b], writes=[kf.b])
                P.op("dve", lambda e: e.scalar_tensor_tensor(out=r.ap, in0=kf.ap, scalar=-CW1, in1=a.ap, op0=ALU.mult, op1=ALU.add),
                     reads=[kf.b, a.b], writes=[r.b])
                P.op("dve", lambda e: e.scalar_tensor_tensor(out=r.ap, in0=kf.ap, scalar=-CW2, in1=r.ap, op0=ALU.mult, op1=ALU.add),
                     reads=[kf.b, r.b], writes=[r.b])
                P.op("dve", lambda e: e.tensor_scalar(out=m.ap, in0=r.ap, scalar1=float(math.pi - shift), scalar2=-TWO_PI,
                                                      op0=ALU.is_gt, op1=ALU.mult), reads=[r.b], writes=[m.b])
                P.op("dve", lambda e: e.scalar_tensor_tensor(out=r.ap, in0=r.ap, scalar=float(shift), in1=m.ap, op0=ALU.add, op1=ALU.add),
                     reads=[r.b, m.b], writes=[r.b])
                P.op("dve", lambda e: e.tensor_scalar(out=m.ap, in0=r.ap, scalar1=float(-math.pi), scalar2=TWO_PI,
                                                      op0=ALU.is_lt, op1=ALU.mult), reads=[r.b], writes=[m.b])
                P.op("dve", lambda e: e.tensor_tensor(out=r.ap, in0=r.ap, in1=m.ap, op=ALU.add), reads=[r.b, m.b], writes=[r.b])
                P.op("dve", lambda e: e.tensor_scalar(out=r.ap, in0=r.ap, scalar1=3.1415925, scalar2=-3.1415925, op0=ALU.min, op1=ALU.max),
                     reads=[r.b], writes=[r.b])
                if signed:
                    P.op("act", lambda e: e.activation(out=dst[:, tsl(t)], in_=r.ap, func=AF.Sin, scale=pcol(P_SGN)),
                         reads=[r.b, PAR.b], writes=[dstb[t]])
                else:
                    P.op("act", lambda e: e.activation(out=dst[:, tsl(t)], in_=r.ap, func=AF.Sin), reads=[r.b], writes=[dstb[t]])

    def colstats(srcs, n, eps):
        ps1 = bank()
        ps2 = bank()
        k = len(srcs)
        for i, (ap16, b, apsq) in enumerate(srcs):
            P.mm_group([lambda pe, ap16=ap16, i=i: pe.matmul(ps1.ap, lhsT=ONESB.ap, rhs=ap16, start=(i == 0), stop=(i == k - 1))],
                       reads=[b, ONESB.b], writes=[ps1.b])
            sq = sqbuf()
            P.op("act", lambda e, apsq=apsq, sq=sq: e.activation(out=sq.ap, in_=apsq, func=AF.Square), reads=[b], writes=[sq.b])
            P.mm_group([lambda pe, sq=sq, i=i: pe.matmul(ps2.ap, lhsT=ONESB.ap, rhs=sq.ap, start=(i == 0), stop=(i == k - 1))],
                       reads=[sq.b, ONESB.b], writes=[ps2.b])
        mean = stbuf()
        P.op("dve", lambda e: e.tensor_scalar(out=mean.ap, in0=ps1.ap, scalar1=1.0 / n, scalar2=None, op0=ALU.mult),
             reads=[ps1.b], writes=[mean.b])
        msq = acbuf()
        P.op("dve", lambda e: e.tensor_tensor(out=msq.ap, in0=mean.ap, in1=mean.ap, op=ALU.mult), reads=[mean.b], writes=[msq.b])
        var = acbuf()
        P.op("dve", lambda e: e.scalar_tensor_tensor(out=var.ap, in0=ps2.ap, scalar=1.0 / n, in1=msq.ap, op0=ALU.mult, op1=ALU.subtract),
             reads=[ps2.b, msq.b], writes=[var.b])
        P.op("dve", lambda e: e.tensor_scalar(out=var.ap, in0=var.ap, scalar1=0.0, scalar2=None, op0=ALU.max), reads=[var.b], writes=[var.b])
        rstd = stbuf()
        P.op("act", lambda e: e.activation(out=rstd.ap, in_=var.ap, func=AF.Ln, bias=eps, scale=1.0), reads=[var.b], writes=[rstd.b])
        P.op("act", lambda e: e.activation(out=rstd.ap, in_=rstd.ap, func=AF.Exp, scale=-0.5), reads=[rstd.b], writes=[rstd.b])
        return mean, rstd

    def out_branch(l, wname, gi, first):
        for cp in range(C):
            w = load_unit([(0, 128, wcols(wname, l, cp * 128, 128)),
                           (128, 128, wcols("w_in", l, O_GATE + gi * 1024 + cp * 128, 128))], C, 256)
            for t in range(T):
                py = bank()
                proj(w, 0, t, py, src=U, srcb=Ub)
                pg = bank()
                proj(w, 128, t, pg)
                sg = acbuf()
                P.op("act", lambda e, sg=sg, pg=pg: e.activation(out=sg.ap, in_=pg.ap, func=AF.Sigmoid), reads=[pg.b], writes=[sg.b])
                if first:
                    P.op("dve", lambda e, sg=sg, py=py: e.tensor_tensor(out=M[:, cp, tsl(t)], in0=py.ap, in1=sg.ap, op=ALU.mult),
                         reads=[py.b, sg.b], writes=[Mb[cp][t]])
                else:
                    tmp = acbuf()
                    P.op("dve", lambda e, sg=sg, py=py, tmp=tmp: e.tensor_tensor(out=tmp.ap, in0=py.ap, in1=sg.ap, op=ALU.mult),
                         reads=[py.b, sg.b], writes=[tmp.b])
                    P.op("dve", lambda e, tmp=tmp: e.tensor_tensor(out=M[:, cp, tsl(t)], in0=M[:, cp, tsl(t)], in1=tmp.ap, op=ALU.add),
                         reads=[tmp.b, Mb[cp][t]], writes=[Mb[cp][t]])

    rope_needed = {"v": True}

    def mixer(l, seq, do_pre, nxt):
        if do_pre:
            prenorm(l, seq, 2)
        COS, SINS, COSb, SINSb = ROPE["COS"], ROPE["SINS"], ROPE["COSb"], ROPE["SINSb"]
        sets = []
        for i in range(2):
            o = i * 8192
            sets.append({"QR": R2b16[:, o:o + 2048], "KR": R2b16[:, o + 2048:o + 4096],
                         "VH": R2b16[:, o + 4096:o + 8192].rearrange("p (a b) -> p a b", a=16),
                         "QRb": [Buf() for _ in range(T)], "KRb": [Buf() for _ in range(T)], "VHb": [Buf() for _ in range(8)]})
        carve(sum([st["QRb"] + st["KRb"] + st["VHb"] for st in sets], []))
        if rope_needed["v"]:
            for t in range(T):
                rope_tile(seq, t)
        rot["banks"] = list(range(6))
        OB = [PS[6], PS[7]]

        def a_weights(hd):
            wq = load_unit([(0, 128, wcols("w_in", l, O_Q + hd * 128, 128)),
                            (128, 64, wcols("w_in", l, O_Q + hd * 128 + 64, 64)),
                            (192, 64, wcols("w_in", l, O_Q + hd * 128, 64))], C, 256)
            wk = load_unit([(0, 128, wcols("w_in", l, O_K + hd * 128, 128)),
                            (128, 64, wcols("w_in", l, O_K + hd * 128 + 64, 64)),
                            (192, 64, wcols("w_in", l, O_K + hd * 128, 64))], C, 256)
            wv = load_unit([(0, 256, wcols("w_in", l, O_V + hd * 256, 256))], C, 256)
            return wq, wk, wv

        def a_proj_tile(st, t, wq, wk):
            for (w, dst, dstb) in ((wq, st["QR"], st["QRb"]), (wk, st["KR"], st["KRb"])):
                p1 = bank()
                proj(w, 0, t, p1)
                p2 = bank()
                proj(w, 128, t, p2)
                t1 = acbuf()
                t2 = acbuf()
                P.op("dve", lambda e, p1=p1, t1=t1: e.tensor_tensor(out=t1.ap, in0=p1.ap, in1=COS[:, tsl(t)], op=ALU.mult),
                     reads=[p1.b, COSb[t]], writes=[t1.b])
                P.op("dve", lambda e, p2=p2, t2=t2: e.tensor_tensor(out=t2.ap, in0=p2.ap, in1=SINS[:, tsl(t)], op=ALU.mult),
                     reads=[p2.b, SINSb[t]], writes=[t2.b])
                P.op("dve", lambda e, t1=t1, t2=t2, dst=dst: e.tensor_tensor(out=dst[:, tsl(t)], in0=t1.ap, in1=t2.ap, op=ALU.add),
                     reads=[t1.b, t2.b], writes=[dstb[t]])

        def a_v(st, ts2, wv):
            VH = st["VH"]
            pv = bank()
            for half in range(2):
                ts = 2 * ts2 + half
                P.mm_group([
                    (lambda pe, k=k: pe.matmul(pv.ap[:, half * 256:(half + 1) * 256], lhsT=H[:, k, ts * 128:(ts + 1) * 128],
                                               rhs=wv.ap[:, k, :], start=(k == 0), stop=(k == C - 1)))
                    for k in range(C)], reads=[wv.b] + [Hb[k][ts // 4] for k in range(C)], writes=[pv.b])
            P.op("act", lambda e, pv=pv: e.activation(out=VH[:, 2 * ts2:2 * ts2 + 2, :], in_=pv.ap.rearrange("p (a b) -> p a b", a=2), func=AF.Copy),
                 reads=[pv.b], writes=[st["VHb"][ts2]])

        aw = {0: a_weights(0)}
        pend = {"f": None}
        for t in range(T):
            a_proj_tile(sets[0], t, aw[0][0], aw[0][1])
            a_v(sets[0], 2 * t, aw[0][2])
            a_v(sets[0], 2 * t + 1, aw[0][2])
        for hd in range(4):
            st = sets[hd % 2]
            QR, KR, VH, QRb, KRb, VHb = st["QR"], st["KR"], st["VH"], st["QRb"], st["KRb"], st["VHb"]
            gamma = 1.0 - 2.0 ** (-5.0 - hd)
            lg = math.log(gamma)
            P.op("act", lambda e: e.activation(out=DTAB.ap, in_=WTAB.ap, func=AF.Exp, scale=float(lg)), reads=[WTAB.b], writes=[DTAB.b])
            wgr = load_unit([(0, 256, wcols("w_in", l, O_GR + hd * 256, 256))], C, 256)
            if hd + 1 < 4:
                aw[hd + 1] = a_weights(hd + 1)
            for ib in range(T):
                njt = 4 * ib + 4
                s_banks = {}

                def issue_s(jt):
                    ps = bank()
                    P.mm_group([lambda pe, ps=ps: pe.matmul(ps.ap, lhsT=KR[:, jt * 128:(jt + 1) * 128], rhs=QR[:, tsl(ib)], start=True, stop=True)],
                               reads=[KRb[jt // 4], QRb[ib]], writes=[ps.b])
                    s_banks[jt] = ps

                issue_s(0)
                issue_s(1)
                for jt in range(njt):
                    if jt + 2 < njt:
                        issue_s(jt + 2)
                    ps = s_banks.pop(jt)
                    pt = ptbuf()
                    r = jt - 4 * ib
                    if r >= 0:
                        tab = DTAB.ap[:, 384 - 128 * r:384 - 128 * r + TT]
                        sc = DK ** -0.5
                    else:
                        tab = DTAB.ap[:, 896:1408]
                        sc = DK ** -0.5 * gamma ** (512 * ib - 128 * jt)
                    P.op("dve", lambda e, ps=ps, pt=pt, tab=tab, sc=sc: e.scalar_tensor_tensor(
                        out=pt.ap, in0=ps.ap, scalar=float(sc), in1=tab, op0=ALU.mult, op1=ALU.mult),
                        reads=[ps.b, DTAB.b], writes=[pt.b])
                    P.mm_group([
                        (lambda pe, e_=e_, pt=pt: pe.matmul(OB[e_].ap, lhsT=VH[:, jt, e_ * 128:(e_ + 1) * 128], rhs=pt.ap,
                                                           start=(jt == 0), stop=(jt == njt - 1)))
                        for e_ in range(2)], reads=[pt.b, VHb[jt // 2]], writes=[OB[0].b, OB[1].b])
                if pend["f"] is not None:
                    pend["f"]()
                    pend["f"] = None
                osb = [stbuf(), stbuf()]
                for e_ in range(2):
                    P.op("act", lambda e, e_=e_: e.activation(out=osb[e_].ap, in_=OB[e_].ap, func=AF.Copy), reads=[OB[e_].b], writes=[osb[e_].b])
                for e_ in range(2):
                    P.op("act", lambda e, e_=e_: e.activation(out=HNB[e_].ap, in_=osb[e_].ap, func=AF.Copy), reads=[osb[e_].b], writes=[HNB[e_].b])
                    P.op("act", lambda e, e_=e_: e.activation(out=HNB[2 + e_].ap, in_=osb[e_].ap, func=AF.Square), reads=[osb[e_].b], writes=[HNB[2 + e_].b])
                if hd + 1 < 4:
                    nst = sets[(hd + 1) % 2]
                    a_proj_tile(nst, ib, aw[hd + 1][0], aw[hd + 1][1])
                    a_v(nst, 2 * ib, aw[hd + 1][2])
                    a_v(nst, 2 * ib + 1, aw[hd + 1][2])
                ps1 = bank()
                ps2 = bank()
                for i in range(2):
                    P.mm_group([lambda pe, i=i: pe.matmul(ps1.ap, lhsT=ONESB.ap, rhs=HNB[i].ap, start=(i == 0), stop=(i == 1))],
                               reads=[HNB[i].b, ONESB.b], writes=[ps1.b])
                for i in range(2):
                    P.mm_group([lambda pe, i=i: pe.matmul(ps2.ap, lhsT=ONESB.ap, rhs=HNB[2 + i].ap, start=(i == 0), stop=(i == 1))],
                               reads=[HNB[2 + i].b, ONESB.b], writes=[ps2.b])
                mean = stbuf()
                P.op("dve", lambda e: e.tensor_scalar(out=mean.ap, in0=ps1.ap, scalar1=1.0 / 256, scalar2=None, op0=ALU.mult),
                     reads=[ps1.b], writes=[mean.b])
                msq = acbuf()
                P.op("dve", lambda e: e.tensor_tensor(out=msq.ap, in0=mean.ap, in1=mean.ap, op=ALU.mult), reads=[mean.b], writes=[msq.b])
                var = acbuf()
                P.op("dve", lambda e: e.scalar_tensor_tensor(out=var.ap, in0=ps2.ap, scalar=1.0 / 256, in1=msq.ap, op0=ALU.mult, op1=ALU.subtract),
                     reads=[ps2.b, msq.b], writes=[var.b])
                P.op("dve", lambda e: e.tensor_scalar(out=var.ap, in0=var.ap, scalar1=0.0, scalar2=None, op0=ALU.max), reads=[var.b], writes=[var.b])
                rstd = stbuf()
                P.op("act", lambda e: e.activation(out=rstd.ap, in_=var.ap, func=AF.Ln, bias=1e-5, scale=1.0), reads=[var.b], writes=[rstd.b])
                P.op("act", lambda e: e.activation(out=rstd.ap, in_=rstd.ap, func=AF.Exp, scale=-0.5), reads=[rstd.b], writes=[rstd.b])
                sgs = []
                for e_ in range(2):
                    pgr = bank()
                    proj(wgr, e_ * 128, ib, pgr)
                    sg = acbuf()
                    P.op("act", lambda e, sg=sg, pgr=pgr: e.activation(out=sg.ap, in_=pgr.ap, func=AF.Silu), reads=[pgr.b], writes=[sg.b])
                    sgs.append(sg)

                def hn_final(hd=hd, ib=ib, osb=osb, mean=mean, rstd=rstd, sgs=sgs):
                    for e_ in range(2):
                        P.op("dve", lambda e, e_=e_: e.tensor_tensor(out=osb[e_].ap, in0=osb[e_].ap, in1=mean.ap, op=ALU.subtract),
                             reads=[osb[e_].b, mean.b], writes=[osb[e_].b])
                        P.op("dve", lambda e, e_=e_: e.tensor_tensor(out=osb[e_].ap, in0=osb[e_].ap, in1=rstd.ap, op=ALU.mult),
                             reads=[osb[e_].b, rstd.b], writes=[osb[e_].b])
                        P.op("dve", lambda e, e_=e_: e.tensor_tensor(out=U[:, 2 * hd + e_, tsl(ib)], in0=osb[e_].ap, in1=sgs[e_].ap, op=ALU.mult),
                             reads=[osb[e_].b, sgs[e_].b], writes=[Ub[2 * hd + e_][ib]])

                pend["f"] = hn_final
        if pend["f"] is not None:
            pend["f"]()
            pend["f"] = None
        rot["banks"] = list(range(8))
        out_branch(l, "w_ret_o", 0, True)

        PB = R2[:, 0:2 + S]
        PBb = [Buf() for _ in range(T)]
        carve(PBb)
        P.op("dve", lambda e: e.memset(PB[:, 0:2], 0.0), writes=[PBb[0]])
        for c in range(C):
            wbc = load_unit([(0, 128, wcols("w_in", l, O_SCB + c * 128, 128)),
                             (128, 128, wcols("w_in", l, O_SCC + c * 128, 128))], C, 256)
            wx = load_unit([(0, 128, wcols("w_in", l, O_SCX + c * 128, 128))], C, 256)
            for t in range(T):
                pc = bank()
                proj(wbc, 128, t, pc)
                px = bank()
                proj(wx, 0, t, px)
                pbk = bank()
                proj(wbc, 0, t, pbk)
                cs = acbuf()
                P.op("act", lambda e, cs=cs, pc=pc: e.activation(out=cs.ap, in_=pc.ap, func=AF.Copy), reads=[pc.b], writes=[cs.b])
                P.op("dve", lambda e, cs=cs, px=px: e.tensor_tensor(out=PB[:, 2 + t * TT:2 + (t + 1) * TT], in0=px.ap, in1=cs.ap, op=ALU.mult),
                     reads=[px.b, cs.b], writes=[PBb[t]])
                cv = acbuf()
                rd = [PBb[t]] + ([PBb[t - 1]] if t > 0 else [])
                wcol = lambda k: pcol(P_SCW + (l * 3 + k) * C + c)
                P.op("dve", lambda e, cv=cv: e.tensor_scalar(out=cv.ap, in0=PB[:, 2 + t * TT:2 + (t + 1) * TT], scalar1=wcol(2), scalar2=None, op0=ALU.mult),
                     reads=rd + [PAR.b], writes=[cv.b])
                P.op("dve", lambda e, cv=cv: e.scalar_tensor_tensor(out=cv.ap, in0=PB[:, 1 + t * TT:1 + (t + 1) * TT], scalar=wcol(1), in1=cv.ap,
                                                                    op0=ALU.mult, op1=ALU.add), reads=rd + [cv.b], writes=[cv.b])
                P.op("dve", lambda e, cv=cv: e.scalar_tensor_tensor(out=cv.ap, in0=PB[:, t * TT:(t + 1) * TT], scalar=wcol(0), in1=cv.ap,
                                                                    op0=ALU.mult, op1=ALU.add), reads=rd + [cv.b], writes=[cv.b])
                P.op("dve", lambda e, cv=cv, pbk=pbk: e.tensor_tensor(out=U[:, c, tsl(t)], in0=pbk.ap, in1=cv.ap, op=ALU.mult),
                     reads=[pbk.b, cv.b], writes=[Ub[c][t]])
        out_branch(l, "w_sc_o", 1, False)

        UG = R2b16[:, 0:30 + S]
        UGb = [Buf() for _ in range(T)]
        DGs = [R2b16[:, 4096 + i * 4096:4096 + i * 4096 + 31 * 128].rearrange("p (k m) -> p k m", k=31) for i in range(2)]
        DGbs = [Buf(), Buf()]
        carve(UGb + DGbs)
        P.op("dve", lambda e: e.memset(UG[:, 0:30], 0.0), writes=[UGb[0]])

        def c_load(c):
            wab = load_unit([(0, 128, wcols("w_in", l, O_GA + c * 128, 128)),
                             (128, 128, wcols("w_in", l, O_GB + c * 128, 128))], C, 256)
            DG, DGb = DGs[c % 2], DGbs[c % 2]
            for k in range(31):
                P.op("dve", lambda e, k=k: e.tensor_scalar(out=DG[:, k, :], in0=IDB.ap, scalar1=pcol(P_CFW + (l * 31 + k) * C + c), scalar2=None, op0=ALU.mult),
                     reads=[IDB.b, PAR.b], writes=[DGb])
            return wab

        def c_proj(c, t, wab):
            pa = bank()
            proj(wab, 0, t, pa)
            pb_ = bank()
            proj(wab, 128, t, pb_)
            sg = acbuf()
            P.op("act", lambda e, sg=sg, pb_=pb_: e.activation(out=sg.ap, in_=pb_.ap, func=AF.Sigmoid), reads=[pb_.b], writes=[sg.b])
            P.op("dve", lambda e, sg=sg, pa=pa: e.tensor_tensor(out=UG[:, 30 + t * TT:30 + (t + 1) * TT], in0=pa.ap, in1=sg.ap, op=ALU.mult),
                 reads=[pa.b, sg.b], writes=[UGb[t]])

        def c_conv(c, t):
            DG, DGb = DGs[c % 2], DGbs[c % 2]
            pcv = bank()
            rd = [UGb[t]] + ([UGb[t - 1]] if t > 0 else [])
            P.mm_group([
                (lambda pe, k=k: pe.matmul(pcv.ap, lhsT=DG[:, k, :], rhs=UG[:, t * TT + k:t * TT + k + TT], start=(k == 0), stop=(k == 30)))
                for k in range(31)], reads=rd + [DGb], writes=[pcv.b])
            P.op("act", lambda e, pcv=pcv: e.activation(out=U[:, c, tsl(t)], in_=pcv.ap, func=AF.Identity, bias=pcol(P_CFB + l * C + c)),
                 reads=[pcv.b, PAR.b], writes=[Ub[c][t]])

        csteps = [(c, t) for c in range(C) for t in range(T)]
        pending = []
        cw = {0: c_load(0)}
        c_proj(0, 0, cw[0])
        c_proj(0, 1, cw[0])
        for i, (c, t) in enumerate(csteps):
            c_conv(c, t)
            j = i + 2
            if j < len(csteps):
                c2, t2 = csteps[j]
                if c2 == c:
                    c_proj(c2, t2, cw[c2])
                else:
                    if c2 not in cw:
                        cw[c2] = c_load(c2)
                    pending.append((c2, t2))
            still = []
            for (c2, t2) in pending:
                need = (c2 - 1, min(t2 + 1, T - 1))
                if csteps.index(need) <= i:
                    c_proj(c2, t2, cw[c2])
                else:
                    still.append((c2, t2))
            pending[:] = still
        for t in range(T):
            mean, rstd = colstats([(U[:, c, tsl(t)], Ub[c][t], U[:, c, tsl(t)]) for c in range(C)], D, 1e-5)
            for c in range(C):
                tmp = acbuf()
                P.op("dve", lambda e, tmp=tmp: e.tensor_tensor(out=tmp.ap, in0=U[:, c, tsl(t)], in1=mean.ap, op=ALU.subtract),
                     reads=[Ub[c][t], mean.b], writes=[tmp.b])
                P.op("dve", lambda e, tmp=tmp: e.scalar_tensor_tensor(out=tmp.ap, in0=tmp.ap, scalar=pcol(P_LNG + l * C + c), in1=rstd.ap,
                                                                      op0=ALU.mult, op1=ALU.mult), reads=[tmp.b, rstd.b, PAR.b], writes=[tmp.b])
                P.op("act", lambda e, tmp=tmp: e.activation(out=U[:, c, tsl(t)], in_=tmp.ap, func=AF.Silu, bias=pcol(P_LNB + l * C + c)),
                     reads=[tmp.b, PAR.b], writes=[Ub[c][t]])
        out_branch(l, "w_cf_o", 2, False)

        WO = TB(R2b16[:, 0:8192].rearrange("p (k n) -> p k n", k=C))
        XS2 = XSt(R2[:, 4096:8192].rearrange("p (c n) -> p c n", c=C))
        carve([WO.b] + XS2.bs)
        for q4 in range(4):
            P.dma("pool", WO.ap[:, :, q4 * 256:(q4 + 1) * 256], wcols("w_o", l, q4 * 256, 256), writes=[WO.b])

        def mt(i, c):
            cc, tt = 4 + 2 * i + c // 4, c % 4
            return ACC[:, cc, tsl(tt)], ACCb[cc][tt]

        for t in range(T):
            for cp in range(C):
                pm = bank()
                proj(WO, cp * 128, t, pm, src=M, srcb=Mb)
                ap, b_ = mt(t % 2, cp)
                if cp % 2 == 0:
                    P.op("act", lambda e, pm=pm, ap=ap: e.activation(out=ap, in_=pm.ap, func=AF.Copy), reads=[pm.b], writes=[b_])
                else:
                    P.op("dve", lambda e, pm=pm, ap=ap: e.tensor_copy(out=ap, in_=pm.ap), reads=[pm.b], writes=[b_])
            postnorm_tile(l, seq, 3, t, lambda c: mt(t % 2, c), 1.0, nxt, XST if t % 2 == 0 else XS2)

    def init_seq(seq):
        XTOK = XST.ap.rearrange("p c n -> p (c n)").rearrange("p (a d) -> p a d", a=4)
        XF = TB(R2[:, 0:4096].rearrange("p (c n) -> p c n", c=C))
        carve([XF.b])
        for t in range(T):
            carve(XST.bs, "XST")
            P.dma("sp", XTOK, x_d[seq, tsl(t), :].rearrange("(a p) d -> p a d", p=128), writes=XST.bs)
            for c in range(C):
                pb = bank()
                P.mm_group([
                    (lambda pe, a=a: pe.transpose(out=pb.ap[:, a * 128:(a + 1) * 128], in_=XTOK[:, a, c * 128:(c + 1) * 128], identity=IDENT))
                    for a in range(4)], reads=XST.bs + [PAR.b], writes=[pb.b])
                if c % 2 == 0:
                    P.op("act", lambda e, pb=pb: e.activation(out=XF.ap[:, c, :], in_=pb.ap, func=AF.Copy), reads=[pb.b], writes=[XF.b])
                else:
                    P.op("dve", lambda e, pb=pb: e.tensor_copy(out=XF.ap[:, c, :], in_=pb.ap), reads=[pb.b], writes=[XF.b])
            P.dma("sp", xs_d[seq, :, :, tsl(t)].rearrange("c p n -> p c n"), XF.ap, reads=[XF.b], writes=[xsb[seq][t]])

    def output_seq(seq):
        OT = TB(R2[:, 0:4096].rearrange("p (a d) -> p a d", a=4))
        carve([OT.b])
        toks = []
        for t in range(T):
            load_x(seq, t)
            for a in range(4):
                for half in range(2):
                    pb = bank()
                    P.mm_group([
                        (lambda pe, c4=c4: pe.transpose(out=pb.ap[:, c4 * 128:(c4 + 1) * 128],
                                                        in_=XST.ap[:, half * 4 + c4, a * 128:(a + 1) * 128], identity=IDENT))
                        for c4 in range(4)], reads=XST.bs + [PAR.b], writes=[pb.b])
                    if half == 0:
                        P.op("act", lambda e, pb=pb: e.activation(out=OT.ap[:, a, 0:512], in_=pb.ap, func=AF.Copy), reads=[pb.b], writes=[OT.b])
                    else:
                        P.op("dve", lambda e, pb=pb: e.tensor_copy(out=OT.ap[:, a, 512:1024], in_=pb.ap), reads=[pb.b], writes=[OT.b])
            toks.append(P.dma("sp", out_d[seq, tsl(t), :].rearrange("(a p) d -> p a d", p=128), OT.ap, reads=[OT.b], writes=[outb[seq][t]]))
        return toks

    all_out = []
    PRE_IDX = {0: 0, 1: 2, 2: 4}
    subs = [(l, sub) for l in range(nl) for sub in range(3)]
    if stop is not None:
        subs = subs[:stop]
    for seq in range(nseq):
        init_seq(seq)
        for i, (l, sub) in enumerate(subs):
            nxt = (subs[i + 1][0], PRE_IDX[subs[i + 1][1]]) if i + 1 < len(subs) else None
            do_pre = (i == 0)
            if sub == 0:
                hooks = None
                rope_needed["v"] = True
                if nxt is not None:
                    hooks = {12 + 8 * t: (lambda t=t: rope_tile(seq, t)) for t in range(T)}
                    rope_needed["v"] = False
                ffn(l, seq, "ffn1_w_gu", "ffn1_w_down", 0, 1, do_pre, nxt, hooks)
            elif sub == 1:
                mixer(l, seq, do_pre, nxt)
                rope_needed["v"] = True
            else:
                ffn(l, seq, "ffn2_w_gu", "ffn2_w_down", 4, 5, do_pre, nxt)
        all_out += output_seq(seq)
    for t in all_out:
        P._wait("sp", t)
    return nc


def pack_consts(norm_g, sc_conv_w, cf_dw_w, cf_dw_b, cf_ln_g, cf_ln_b):
    par = np.zeros((128, NPAR), np.float32)

    def fm(a):
        a = np.asarray(a, np.float32)
        lead = a.shape[:-1]
        a = a.reshape(lead + (C, 128))
        a = np.moveaxis(a, -1, 0)
        return a.reshape(128, -1)

    par[:, P_NG:P_NG + L * 6 * C] = fm(norm_g)
    par[:, P_SCW:P_SCW + L * 3 * C] = fm(sc_conv_w)
    par[:, P_CFW:P_CFW + L * 31 * C] = fm(cf_dw_w)
    par[:, P_CFB:P_CFB + L * C] = fm(cf_dw_b)
    par[:, P_LNG:P_LNG + L * C] = fm(cf_ln_g)
    par[:, P_LNB:P_LNB + L * C] = fm(cf_ln_b)
    invf = (10000.0 ** (-np.arange(64, dtype=np.float32) / 64.0)).astype(np.float32)
    par[:, P_INVF] = np.concatenate([invf, invf])
    par[:64, P_SGN] = -1.0
    par[64:, P_SGN] = 1.0
    par[:, P_ID:P_ID + 128] = np.eye(128, dtype=np.float32)
    j = np.arange(128)[:, None]
    x = np.arange(-384, 512)[None, :]
    ok = (x >= 0) & ((j // 64) <= (np.floor_divide(x, 64)))
    wn = np.where(ok, np.abs(x - j), BIGD).astype(np.float32)
    xf = np.arange(512)[None, :]
    wf = (xf - j).astype(np.float32)
    wtab = np.concatenate([wn, wf], axis=1).astype(np.float32)
    return par, wtab


_NC_CACHE = {}


def kernel(x, positions, norm_g, ffn1_w_gu, ffn1_w_down, w_in, w_ret_o, sc_conv_w, w_sc_o,
           cf_dw_w, cf_dw_b, cf_ln_g, cf_ln_b, w_cf_o, w_o, ffn2_w_gu, ffn2_w_down):
    x = np.ascontiguousarray(np.asarray(x, np.float32))
    positions = np.ascontiguousarray(np.asarray(positions, np.int32))
    par, wtab = pack_consts(np.asarray(norm_g), np.asarray(sc_conv_w), np.asarray(cf_dw_w), np.asarray(cf_dw_b),
                            np.asarray(cf_ln_g), np.asarray(cf_ln_b))
    ws = {"ffn1_w_gu": ffn1_w_gu, "ffn1_w_down": ffn1_w_down, "w_in": w_in, "w_ret_o": w_ret_o, "w_sc_o": w_sc_o,
          "w_cf_o": w_cf_o, "w_o": w_o, "ffn2_w_gu": ffn2_w_gu, "ffn2_w_down": ffn2_w_down}
    ws = {k: np.ascontiguousarray(np.asarray(v, np.float32)) for k, v in ws.items()}
    if "nc" not in _NC_CACHE:
        _NC_CACHE["nc"] = build()
    nc = _NC_CACHE["nc"]
    in_maps = []
    for i in range(NCORES):
        m = {"x": x[NSEQ * i:NSEQ * (i + 1)], "pos": positions[NSEQ * i:NSEQ * (i + 1)], "par": par, "wtab": wtab}
        m.update(ws)
        in_maps.append(m)
    res = run_bass_kernel_spmd(nc, in_maps, core_ids=list(range(NCORES)))
    return np.concatenate([np.asarray(r["out"]) for r in res.results], axis=0).astype(np.float32)
```

```python
import math
import numpy as np
import concourse.bass as bass
import concourse.mybir as mybir
from concourse.bass_utils import run_bass_kernel_spmd

F32 = mybir.dt.float32
BF16 = mybir.dt.bfloat16
I32 = mybir.dt.int32
AF = mybir.ActivationFunctionType
ALU = mybir.AluOpType

S = 2048
D = 1024
C = 8
T = 4
TT = 512
NSEQ = 2
L = 2
DFF = 4096
DIN = 11264
NCORES = 8
DK = 128
O_Q, O_K, O_V, O_GR, O_SCB, O_SCC, O_SCX, O_GA, O_GB, O_GATE = 0, 512, 1024, 2048, 3072, 4096, 5120, 6144, 7168, 8192
BIGD = 1.0e5

P_NG = 0
P_SCW = P_NG + L * 6 * C
P_CFW = P_SCW + L * 3 * C
P_CFB = P_CFW + L * 31 * C
P_LNG = P_CFB + L * C
P_LNB = P_LNG + L * C
P_INVF = P_LNB + L * C
P_SGN = P_INVF + 1
P_ID = P_SGN + 1
NPAR = P_ID + 128

WNAMES = ["ffn1_w_gu", "ffn1_w_down", "w_in", "w_ret_o", "w_sc_o", "w_cf_o", "w_o", "ffn2_w_gu", "ffn2_w_down"]
WSHAPES = {"ffn1_w_gu": [L, D, 2 * DFF], "ffn1_w_down": [L, DFF, D], "w_in": [L, D, DIN],
           "w_ret_o": [L, D, D], "w_sc_o": [L, D, D], "w_cf_o": [L, D, D], "w_o": [L, D, D],
           "ffn2_w_gu": [L, D, 2 * DFF], "ffn2_w_down": [L, DFF, D]}


class Buf:
    __slots__ = ("w", "r", "dsem", "dcnt")

    def __init__(self):
        self.w = None
        self.r = {}
        self.dsem = None
        self.dcnt = 0


class TB:
    __slots__ = ("ap", "b")

    def __init__(self, ap, b=None):
        self.ap = ap
        self.b = b if b is not None else Buf()


class XSt:
    __slots__ = ("ap", "bs")

    def __init__(self, ap):
        self.ap = ap
        self.bs = [Buf() for _ in range(C)]


class Prog:
    def __init__(self, nc):
        self.nc = nc
        self.engs = {"pe": nc.tensor, "dve": nc.vector, "act": nc.scalar, "pool": nc.gpsimd, "sp": nc.sync}
        self.sem = {k: nc.alloc_semaphore(f"s_{k}") for k in self.engs}
        self.cnt = {k: 0 for k in self.engs}
        self.waited = {k: {} for k in self.engs}
        self.nsem = 0
        self.dma_toks = []

    def _wait(self, e, tok):
        sem, val = tok
        if e == "pe" and sem is self.sem["pe"]:
            return
        key = id(sem)
        if self.waited[e].get(key, 0) >= val:
            return
        self.engs[e].wait_ge(sem, val)
        self.waited[e][key] = val

    def deps(self, e, reads, writes):
        for b in reads:
            if b.w is not None:
                self._wait(e, b.w)
        for b in writes:
            if b.w is not None:
                self._wait(e, b.w)
            for t in b.r.values():
                self._wait(e, t)

    def commit(self, tok, reads, writes):
        k = id(tok[0])
        for b in reads:
            o = b.r.get(k)
            if o is None or o[1] < tok[1]:
                b.r[k] = tok
        for b in writes:
            b.w = tok
            b.r = {}

    def op(self, e, fn, reads=(), writes=()):
        self.deps(e, reads, writes)
        ins = fn(self.engs[e])
        self.cnt[e] += 1
        ins.then_inc(self.sem[e], 1)
        tok = (self.sem[e], self.cnt[e])
        self.commit(tok, reads, writes)
        return tok

    def mm_group(self, fns, reads=(), writes=()):
        e = "pe"
        self.deps(e, reads, writes)
        ins = None
        for fn in fns:
            ins = fn(self.engs[e])
        self.cnt[e] += 1
        ins.then_inc(self.sem[e], 1)
        tok = (self.sem[e], self.cnt[e])
        self.commit(tok, reads, writes)
        return tok

    def dma(self, q, out, in_, reads=(), writes=()):
        self.deps(q, reads, writes)
        tgt = writes[0] if writes else reads[0]
        if tgt.dsem is None:
            tgt.dsem = self.nc.alloc_semaphore(f"d{self.nsem}")
            self.nsem += 1
        tgt.dcnt += 16
        self.engs[q].dma_start(out=out, in_=in_).then_inc(tgt.dsem, 16)
        tok = (tgt.dsem, tgt.dcnt)
        self.commit(tok, reads, writes)
        self.dma_toks.append(tok)
        return tok

    def retire(self, bufs):
        toks = []
        for b in bufs:
            if b.w is not None:
                toks.append(b.w)
            toks += list(b.r.values())
        for e in self.engs:
            for t in toks:
                self._wait(e, t)

    def barrier(self):
        toks = [(self.sem[k], self.cnt[k]) for k in self.engs if self.cnt[k] > 0] + self.dma_toks
        for e in self.engs:
            for t in toks:
                self._wait(e, t)
        self.dma_toks = []


def build(nl=L, nseq=NSEQ, stop=None):
    nc = bass.Bass("TRN2", target_bir_lowering=False)
    P = Prog(nc)

    def din(name, shape, dt=F32):
        return nc.dram_tensor(name, shape, dt, kind="ExternalInput").ap()

    x_d = din("x", [NSEQ, S, D])
    pos_d = din("pos", [NSEQ, S], I32)
    par_d = din("par", [128, NPAR])
    wtab_d = din("wtab", [128, 1408])
    wd = {n: din(n, WSHAPES[n]) for n in WNAMES}
    out_d = nc.dram_tensor("out", [NSEQ, S, D], F32, kind="ExternalOutput").ap()
    xs_d = nc.dram_tensor("xs", [NSEQ, C, 128, S], F32, kind="Internal").ap()
    xsb = [[Buf() for _ in range(T)] for _ in range(NSEQ)]
    outb = [[Buf() for _ in range(T)] for _ in range(NSEQ)]

    def sb(name, shape, dt):
        return nc.alloc_sbuf_tensor(name, shape, dt).ap()

    H = sb("H", [128, C, S], BF16)
    Hb = [[Buf() for _ in range(T)] for _ in range(C)]
    RA = sb("RA", [128, 16384], F32)
    ACC = RA.rearrange("p (c n) -> p c n", c=C)
    RAb16 = RA.bitcast(BF16)
    M = RAb16[:, 0:16384].rearrange("p (c n) -> p c n", c=C)
    U = RAb16[:, 16384:32768].rearrange("p (c n) -> p c n", c=C)
    ACCb = [[Buf() for _ in range(T)] for _ in range(C)]
    Mb = [[ACCb[c // 2][(c % 2) * 2 + t // 2] for t in range(T)] for c in range(C)]
    Ub = [[ACCb[4 + c // 2][(c % 2) * 2 + t // 2] for t in range(T)] for c in range(C)]
    R2 = sb("R2", [128, 8192], F32)
    R2b16 = R2.bitcast(BF16)
    NSLOT = 6
    RING = sb("RING", [128, NSLOT, 2048], BF16)
    ringb = [Buf() for _ in range(NSLOT)]
    XST = XSt(sb("XST", [128, C, TT], F32))
    PAR = TB(sb("PAR", [128, NPAR], F32))
    WTAB = TB(sb("WTAB", [128, 1408], F32))
    DTAB = TB(sb("DTAB", [128, 1408], F32))
    IDB = TB(sb("IDB", [128, 128], BF16))
    ONESB = TB(sb("ONESB", [128, 128], BF16))
    SQ = [TB(sb(f"SQ{i}", [128, TT], BF16)) for i in range(2)]
    HNB = [TB(sb(f"HNB{i}", [128, TT], BF16)) for i in range(4)]
    ST = [TB(sb(f"ST{i}", [128, TT], F32)) for i in range(4)]
    AC = [TB(sb(f"AC{i}", [128, TT], F32)) for i in range(3)]
    ACTB_raw = [sb(f"ACTB{i}", [128, 2, TT], BF16) for i in range(2)]
    ACTBb = [[Buf(), Buf()] for _ in range(2)]
    PT = [TB(ACTB_raw[0][:, 0, :], ACTBb[0][0]), TB(ACTB_raw[0][:, 1, :], ACTBb[0][1]), TB(ACTB_raw[1][:, 0, :], ACTBb[1][0])]
    PS = [TB(nc.alloc_psum_tensor(f"ps{i}", [128, TT], F32).ap()) for i in range(8)]

    all_out = []
    ctr = {"bank": 0, "sq": 0, "st": 0, "ac": 0, "pt": 0, "ring": 0}
    rot = {"banks": list(range(8))}

    def bank():
        lst = rot["banks"]
        i = lst[ctr["bank"] % len(lst)]
        ctr["bank"] += 1
        return PS[i]

    def rotbuf(lst, key):
        i = ctr[key] % len(lst)
        ctr[key] += 1
        return lst[i]

    def sqbuf():
        return rotbuf(SQ, "sq")

    def stbuf():
        return rotbuf(ST, "st")

    def acbuf():
        return rotbuf(AC, "ac")

    def ptbuf():
        return rotbuf(PT, "pt")

    def pcol(off):
        return PAR.ap[:, off:off + 1]

    def tsl(t):
        return slice(t * TT, (t + 1) * TT)

    IDENT = PAR.ap[:, P_ID:P_ID + 128]

    P.dma("sp", PAR.ap, par_d, writes=[PAR.b])
    P.dma("sp", WTAB.ap, wtab_d, writes=[WTAB.b])
    P.op("dve", lambda e: e.tensor_copy(out=IDB.ap, in_=IDENT), reads=[PAR.b], writes=[IDB.b])
    P.op("dve", lambda e: e.memset(ONESB.ap, 1.0), writes=[ONESB.b])

    def wslot():
        i = ctr["ring"] % NSLOT
        ctr["ring"] += 1
        return i

    def wcols(name, l, c0, n):
        return wd[name][l].rearrange("(k p) n -> p k n", p=128)[:, :, c0:c0 + n]

    def load_unit(pieces, a, bcols):
        i = wslot()
        view = RING[:, i, 0:a * bcols].rearrange("p (a b) -> p a b", a=a)
        for (c0, n, src) in pieces:
            P.dma("pool", view[:, :, c0:c0 + n], src, writes=[ringb[i]])
        return TB(view, ringb[i])

    def proj(w, col0, t, pb, src=H, srcb=Hb):
        P.mm_group([
            (lambda pe, k=k: pe.matmul(pb.ap, lhsT=w.ap[:, k, col0:col0 + 128], rhs=src[:, k, tsl(t)],
                                       start=(k == 0), stop=(k == C - 1)))
            for k in range(C)], reads=[w.b] + [srcb[k][t] for k in range(C)], writes=[pb.b])

    def rstd_from_ssq(pb, n, eps, mul=1.0):
        sd = stbuf()
        P.op("act", lambda e: e.activation(out=sd.ap, in_=pb.ap, func=AF.Ln, bias=eps, scale=1.0 / n),
             reads=[pb.b], writes=[sd.b])
        if mul == 1.0:
            P.op("act", lambda e: e.activation(out=sd.ap, in_=sd.ap, func=AF.Exp, scale=-0.5), reads=[sd.b], writes=[sd.b])
        else:
            P.op("act", lambda e: e.activation(out=sd.ap, in_=sd.ap, func=AF.Exp, scale=-0.5, bias=float(math.log(mul))), reads=[sd.b], writes=[sd.b])
        return sd

    def ssq_bank(srcs):
        pb = bank()
        n = len(srcs)
        for i, (ap, b) in enumerate(srcs):
            sq = sqbuf()
            P.op("act", lambda e, ap=ap, sq=sq: e.activation(out=sq.ap, in_=ap, func=AF.Square), reads=[b], writes=[sq.b])
            P.mm_group([lambda pe, sq=sq, i=i: pe.matmul(pb.ap, lhsT=ONESB.ap, rhs=sq.ap, start=(i == 0), stop=(i == n - 1))],
                       reads=[sq.b, ONESB.b], writes=[pb.b])
        return pb

    live = {"R2": [], "XST": []}

    def carve(new_bufs, region="R2"):
        if live[region] and set(map(id, live[region])) == set(map(id, new_bufs)):
            return
        toks = {}
        for b in live[region]:
            for tok in ([b.w] if b.w is not None else []) + list(b.r.values()):
                k = id(tok[0])
                if k not in toks or toks[k][1] < tok[1]:
                    toks[k] = tok
        for b in new_bufs:
            for k, tok in toks.items():
                if k not in b.r or b.r[k][1] < tok[1]:
                    b.r[k] = tok
        live[region] = list(new_bufs)

    def load_x(seq, t, xst=None):
        xst = xst or XST
        if xst is XST:
            carve(XST.bs, "XST")
        P.dma("sp", xst.ap, xs_d[seq, :, :, tsl(t)].rearrange("c p n -> p c n"), reads=[xsb[seq][t]], writes=xst.bs)

    def store_x(seq, t, xst=None):
        xst = xst or XST
        P.dma("sp", xs_d[seq, :, :, tsl(t)].rearrange("c p n -> p c n"), xst.ap, reads=xst.bs, writes=[xsb[seq][t]])

    def h_from_xst(l, ni, t, xst=None):
        xst = xst or XST
        pb = ssq_bank([(xst.ap[:, c, :], xst.bs[c]) for c in range(C)])
        rs = rstd_from_ssq(pb, D, 1e-6)
        for c in range(C):
            P.op("dve", lambda e, c=c: e.scalar_tensor_tensor(out=H[:, c, tsl(t)], in0=xst.ap[:, c, :],
                                                              scalar=pcol(P_NG + (l * 6 + ni) * C + c), in1=rs.ap,
                                                              op0=ALU.mult, op1=ALU.mult),
                 reads=[xst.bs[c], rs.b, PAR.b], writes=[Hb[c][t]])

    def prenorm(l, seq, ni):
        for t in range(T):
            load_x(seq, t)
            h_from_xst(l, ni, t)

    def postnorm_tile(l, seq, ni, t, rfn, alpha, nxt=None, xst=None, store=True):
        xst = xst or XST
        load_x(seq, t, xst)
        pb = ssq_bank([rfn(c) for c in range(C)])
        rs = rstd_from_ssq(pb, D, 1e-6, mul=alpha)
        for c in range(C):
            ap, b = rfn(c)
            tmp = acbuf()
            P.op("dve", lambda e, ap=ap, tmp=tmp, c=c: e.scalar_tensor_tensor(
                out=tmp.ap, in0=ap, scalar=pcol(P_NG + (l * 6 + ni) * C + c), in1=rs.ap, op0=ALU.mult, op1=ALU.mult),
                reads=[b, rs.b, PAR.b], writes=[tmp.b])
            P.op("dve", lambda e, tmp=tmp, c=c: e.tensor_tensor(out=xst.ap[:, c, :], in0=tmp.ap, in1=xst.ap[:, c, :], op=ALU.add),
                 reads=[tmp.b, xst.bs[c]], writes=[xst.bs[c]])
        if store:
            store_x(seq, t, xst)
        if nxt is not None:
            h_from_xst(nxt[0], nxt[1], t, xst)

    def ffn(l, seq, wgu, wdn, npre, npost, do_pre, nxt, hooks=None, final=False, next_seq=None):
        if do_pre:
            prenorm(l, seq, npre)
        NG = DFF // 256
        hooks = hooks or {}

        def gate_up(i, g, t, wg, wu):
            ai = i % 2
            for jj in range(2):
                pg = bank()
                proj(wg, jj * 128, t, pg)
                pu = bank()
                proj(wu, jj * 128, t, pu)
                sg = acbuf()
                P.op("act", lambda e, sg=sg, pg=pg: e.activation(out=sg.ap, in_=pg.ap, func=AF.Silu),
                     reads=[pg.b], writes=[sg.b])
                P.op("dve", lambda e, sg=sg, pu=pu, jj=jj: e.tensor_tensor(out=ACTB_raw[ai][:, jj, :], in0=pu.ap, in1=sg.ap, op=ALU.mult),
                     reads=[pu.b, sg.b], writes=[ACTBb[ai][jj]])

        def down(i, g, t, wdv):
            ai = i % 2
            for c in range(C):
                pd = bank()
                P.mm_group([
                    (lambda pe, jj=jj: pe.matmul(pd.ap, lhsT=wdv.ap[:, jj, c * 128:(c + 1) * 128], rhs=ACTB_raw[ai][:, jj, :],
                                                 start=(jj == 0), stop=(jj == 1)))
                    for jj in range(2)], reads=[wdv.b, ACTBb[ai][0], ACTBb[ai][1]], writes=[pd.b])
                if g == 0:
                    P.op("act", lambda e, pd=pd: e.activation(out=ACC[:, c, tsl(t)], in_=pd.ap, func=AF.Copy),
                         reads=[pd.b], writes=[ACCb[c][t]])
                else:
                    P.op("dve", lambda e, pd=pd: e.tensor_tensor(out=ACC[:, c, tsl(t)], in0=pd.ap, in1=ACC[:, c, tsl(t)], op=ALU.add),
                         reads=[pd.b, ACCb[c][t]], writes=[ACCb[c][t]])

        def load_group(g):
            j0 = 2 * g
            wg = load_unit([(0, 256, wcols(wgu, l, j0 * 128, 256))], C, 256)
            wu = load_unit([(0, 256, wcols(wgu, l, DFF + j0 * 128, 256))], C, 256)
            wdv = load_unit([(0, 1024, wd[wdn][l][j0 * 128:(j0 + 2) * 128, :].rearrange("(j p) n -> p j n", p=128))], 2, 1024)
            return wg, wu, wdv

        steps = [(g, t) for t in range(T) for g in (0, 1)]
        steps += [(g, t) for g in range(2, NG - 2) for t in range(T)]
        steps += [(g, t) for t in range(T) for g in (NG - 2, NG - 1)]
        wts = {0: load_group(0)}
        left = {g: T for g in range(NG)}
        gate_up(0, steps[0][0], steps[0][1], wts[0][0], wts[0][1])
        xs2 = [XSt(R2[:, 0:4096].rearrange("p (c n) -> p c n", c=C)), XSt(R2[:, 4096:8192].rearrange("p (c n) -> p c n", c=C))]
        for i, (g, t) in enumerate(steps):
            if i + 1 < len(steps):
                g2, t2 = steps[i + 1]
                if g2 not in wts:
                    wts[g2] = load_group(g2)
                gate_up(i + 1, g2, t2, wts[g2][0], wts[g2][1])
            down(i, g, t, wts[g][2])
            left[g] -= 1
            if left[g] == 0:
                wts.pop(g)
            if i in hooks:
                hooks[i]()
            if g == NG - 1:
                if t == 0:
                    carve(xs2[0].bs + xs2[1].bs)
                postnorm_tile(l, seq, npost, t, lambda c: (ACC[:, c, tsl(t)], ACCb[c][t]), 0.5, nxt, xs2[t % 2], store=not final)
                if final:
                    all_out.append(out_tile(seq, t, xs2[t % 2]))
                if next_seq is not None:
                    load_x(next_seq, t, xs2[(t + 1) % 2])
                    h_from_xst(0, 0, t, xs2[(t + 1) % 2])

    TWO_PI = 2.0 * math.pi
    CW1 = 6.28125
    CW2 = TWO_PI - CW1

    XSTF = XST.ap.rearrange("p c n -> p (c n)")
    ROPE = {"COS": XSTF[:, 0:2048], "SINS": XSTF[:, 2048:4096], "COSb": [Buf() for _ in range(T)], "SINSb": [Buf() for _ in range(T)]}

    def rope_tile(seq, t):
        COS, SINS, COSb, SINSb = ROPE["COS"], ROPE["SINS"], ROPE["COSb"], ROPE["SINSb"]
        carve(COSb + SINSb, "XST")
        if True:
            pi_ = stbuf()
            a = stbuf()
            ki = stbuf()
            kf = stbuf()
            P.dma("sp", pi_.ap.bitcast(I32), pos_d[seq:seq + 1, tsl(t)].partition_broadcast(128), writes=[pi_.b])
            P.op("dve", lambda e: e.tensor_copy(out=a.ap, in_=pi_.ap.bitcast(I32)), reads=[pi_.b], writes=[a.b])
            P.op("dve", lambda e: e.tensor_scalar(out=a.ap, in0=a.ap, scalar1=pcol(P_INVF), scalar2=None, op0=ALU.mult),
                 reads=[a.b, PAR.b], writes=[a.b])
            for (shift, dst, dstb, signed) in ((math.pi / 2, COS, COSb, False), (0.0, SINS, SINSb, True)):
                r = acbuf()
                m = acbuf()
                P.op("dve", lambda e: e.tensor_scalar(out=ki.ap.bitcast(I32), in0=a.ap, scalar1=1.0 / TWO_PI, scalar2=shift / TWO_PI,
                                                      op0=ALU.mult, op1=ALU.add), reads=[a.b], writes=[ki.b])
                P.op("dve", lambda e: e.tensor_copy(out=kf.ap, in_=ki.ap.bitcast(I32)), reads=[ki.b], writes=[kf.b])
                P.op("dve", lambda e: e.scalar_tensor_tensor(out=r.ap, in0=kf.ap, scalar=-CW1, in1=a.ap, op0=ALU.mult, op1=ALU.add),
                     reads=[kf.b, a.b], writes=[r.b])
                P.op("dve", lambda e: e.scalar_tensor_tensor(out=r.ap, in0=kf.ap, scalar=-CW2, in1=r.ap, op0=ALU.mult, op1=ALU.add),
                     reads=[kf.b, r.b], writes=[r.b])
                P.op("dve", lambda e: e.tensor_scalar(out=m.ap, in0=r.ap, scalar1=float(math.pi - shift), scalar2=-TWO_PI,
                                                      op0=ALU.is_gt, op1=ALU.mult), reads=[r.b], writes=[m.b])
                P.op("dve", lambda e: e.scalar_tensor_tensor(out=r.ap, in0=r.ap, scalar=float(shift), in1=m.ap, op0=ALU.add, op1=ALU.add),
                     reads=[r.b, m.b], writes=[r.b])
                P.op("dve", lambda e: e.tensor_scalar(out=m.ap, in0=r.ap, scalar1=float(-math.pi), scalar2=TWO_PI,
                                                      op0=ALU.is_lt, op1=ALU.mult), reads=[r.b], writes=[m.b])
                P.op("dve", lambda e: e.tensor_tensor(out=r.ap, in0=r.ap, in1=m.ap, op=ALU.add), reads=[r.b, m.b], writes=[r.b])
                P.op("dve", lambda e: e.tensor_scalar(out=r.ap, in0=r.ap, scalar1=3.1415925, scalar2=-3.1415925, op0=ALU.min, op1=ALU.max),
                     reads=[r.b], writes=[r.b])
                if signed:
                    P.op("act", lambda e: e.activation(out=dst[:, tsl(t)], in_=r.ap, func=AF.Sin, scale=pcol(P_SGN)),
                         reads=[r.b, PAR.b], writes=[dstb[t]])
                else:
                    P.op("act", lambda e: e.activation(out=dst[:, tsl(t)], in_=r.ap, func=AF.Sin), reads=[r.b], writes=[dstb[t]])

    def colstats(srcs, n, eps):
        ps1 = bank()
        ps2 = bank()
        k = len(srcs)
        for i, (ap16, b, apsq) in enumerate(srcs):
            P.mm_group([lambda pe, ap16=ap16, i=i: pe.matmul(ps1.ap, lhsT=ONESB.ap, rhs=ap16, start=(i == 0), stop=(i == k - 1))],
                       reads=[b, ONESB.b], writes=[ps1.b])
            sq = sqbuf()
            P.op("act", lambda e, apsq=apsq, sq=sq: e.activation(out=sq.ap, in_=apsq, func=AF.Square), reads=[b], writes=[sq.b])
            P.mm_group([lambda pe, sq=sq, i=i: pe.matmul(ps2.ap, lhsT=ONESB.ap, rhs=sq.ap, start=(i == 0), stop=(i == k - 1))],
                       reads=[sq.b, ONESB.b], writes=[ps2.b])
        mean = stbuf()
        P.op("dve", lambda e: e.tensor_scalar(out=mean.ap, in0=ps1.ap, scalar1=1.0 / n, scalar2=None, op0=ALU.mult),
             reads=[ps1.b], writes=[mean.b])
        msq = acbuf()
        P.op("dve", lambda e: e.tensor_tensor(out=msq.ap, in0=mean.ap, in1=mean.ap, op=ALU.mult), reads=[mean.b], writes=[msq.b])
        var = acbuf()
        P.op("dve", lambda e: e.scalar_tensor_tensor(out=var.ap, in0=ps2.ap, scalar=1.0 / n, in1=msq.ap, op0=ALU.mult, op1=ALU.subtract),
             reads=[ps2.b, msq.b], writes=[var.b])
        P.op("dve", lambda e: e.tensor_scalar(out=var.ap, in0=var.ap, scalar1=0.0, scalar2=None, op0=ALU.max), reads=[var.b], writes=[var.b])
        rstd = stbuf()
        P.op("act", lambda e: e.activation(out=rstd.ap, in_=var.ap, func=AF.Ln, bias=eps, scale=1.0), reads=[var.b], writes=[rstd.b])
        P.op("act", lambda e: e.activation(out=rstd.ap, in_=rstd.ap, func=AF.Exp, scale=-0.5), reads=[rstd.b], writes=[rstd.b])
        return mean, rstd

    def out_branch(l, wname, gi, first):
        for cp in range(C):
            w = load_unit([(0, 128, wcols(wname, l, cp * 128, 128)),
                           (128, 128, wcols("w_in", l, O_GATE + gi * 1024 + cp * 128, 128))], C, 256)
            for t in range(T):
                py = bank()
                proj(w, 0, t, py, src=U, srcb=Ub)
                pg = bank()
                proj(w, 128, t, pg)
                sg = acbuf()
                P.op("act", lambda e, sg=sg, pg=pg: e.activation(out=sg.ap, in_=pg.ap, func=AF.Sigmoid), reads=[pg.b], writes=[sg.b])
                if first:
                    P.op("dve", lambda e, sg=sg, py=py: e.tensor_tensor(out=M[:, cp, tsl(t)], in0=py.ap, in1=sg.ap, op=ALU.mult),
                         reads=[py.b, sg.b], writes=[Mb[cp][t]])
                else:
                    tmp = acbuf()
                    P.op("dve", lambda e, sg=sg, py=py, tmp=tmp: e.tensor_tensor(out=tmp.ap, in0=py.ap, in1=sg.ap, op=ALU.mult),
                         reads=[py.b, sg.b], writes=[tmp.b])
                    P.op("dve", lambda e, tmp=tmp: e.tensor_tensor(out=M[:, cp, tsl(t)], in0=M[:, cp, tsl(t)], in1=tmp.ap, op=ALU.add),
                         reads=[tmp.b, Mb[cp][t]], writes=[Mb[cp][t]])

    rope_needed = {"v": True}

    def mixer(l, seq, do_pre, nxt):
        if do_pre:
            prenorm(l, seq, 2)
        COS, SINS, COSb, SINSb = ROPE["COS"], ROPE["SINS"], ROPE["COSb"], ROPE["SINSb"]
        sets = []
        for i in range(2):
            o = i * 8192
            sets.append({"QR": R2b16[:, o:o + 2048], "KR": R2b16[:, o + 2048:o + 4096],
                         "VH": R2b16[:, o + 4096:o + 8192].rearrange("p (a b) -> p a b", a=16),
                         "QRb": [Buf() for _ in range(T)], "KRb": [Buf() for _ in range(T)], "VHb": [Buf() for _ in range(8)]})
        carve(sum([st["QRb"] + st["KRb"] + st["VHb"] for st in sets], []))
        if rope_needed["v"]:
            for t in range(T):
                rope_tile(seq, t)
        rot["banks"] = list(range(6))
        OB = [PS[6], PS[7]]

        def a_weights(hd):
            wq = load_unit([(0, 128, wcols("w_in", l, O_Q + hd * 128, 128)),
                            (128, 64, wcols("w_in", l, O_Q + hd * 128 + 64, 64)),
                            (192, 64, wcols("w_in", l, O_Q + hd * 128, 64))], C, 256)
            wk = load_unit([(0, 128, wcols("w_in", l, O_K + hd * 128, 128)),
                            (128, 64, wcols("w_in", l, O_K + hd * 128 + 64, 64)),
                            (192, 64, wcols("w_in", l, O_K + hd * 128, 64))], C, 256)
            wv = load_unit([(0, 256, wcols("w_in", l, O_V + hd * 256, 256))], C, 256)
            return wq, wk, wv

        def a_proj_tile(st, t, wq, wk):
            for (w, dst, dstb) in ((wq, st["QR"], st["QRb"]), (wk, st["KR"], st["KRb"])):
                p1 = bank()
                proj(w, 0, t, p1)
                p2 = bank()
                proj(w, 128, t, p2)
                t1 = acbuf()
                t2 = acbuf()
                P.op("dve", lambda e, p1=p1, t1=t1: e.tensor_tensor(out=t1.ap, in0=p1.ap, in1=COS[:, tsl(t)], op=ALU.mult),
                     reads=[p1.b, COSb[t]], writes=[t1.b])
                P.op("dve", lambda e, p2=p2, t2=t2: e.tensor_tensor(out=t2.ap, in0=p2.ap, in1=SINS[:, tsl(t)], op=ALU.mult),
                     reads=[p2.b, SINSb[t]], writes=[t2.b])
                P.op("dve", lambda e, t1=t1, t2=t2, dst=dst: e.tensor_tensor(out=dst[:, tsl(t)], in0=t1.ap, in1=t2.ap, op=ALU.add),
                     reads=[t1.b, t2.b], writes=[dstb[t]])

        def a_v(st, ts2, wv):
            VH = st["VH"]
            pv = bank()
            for half in range(2):
                ts = 2 * ts2 + half
                P.mm_group([
                    (lambda pe, k=k: pe.matmul(pv.ap[:, half * 256:(half + 1) * 256], lhsT=H[:, k, ts * 128:(ts + 1) * 128],
                                               rhs=wv.ap[:, k, :], start=(k == 0), stop=(k == C - 1)))
                    for k in range(C)], reads=[wv.b] + [Hb[k][ts // 4] for k in range(C)], writes=[pv.b])
            P.op("act", lambda e, pv=pv: e.activation(out=VH[:, 2 * ts2:2 * ts2 + 2, :], in_=pv.ap.rearrange("p (a b) -> p a b", a=2), func=AF.Copy),
                 reads=[pv.b], writes=[st["VHb"][ts2]])

        aw = {0: a_weights(0)}
        pend = {"f": None}
        for t in range(T):
            a_proj_tile(sets[0], t, aw[0][0], aw[0][1])
            a_v(sets[0], 2 * t, aw[0][2])
            a_v(sets[0], 2 * t + 1, aw[0][2])
        for hd in range(4):
            st = sets[hd % 2]
            QR, KR, VH, QRb, KRb, VHb = st["QR"], st["KR"], st["VH"], st["QRb"], st["KRb"], st["VHb"]
            gamma = 1.0 - 2.0 ** (-5.0 - hd)
            lg = math.log(gamma)
            P.op("act", lambda e: e.activation(out=DTAB.ap, in_=WTAB.ap, func=AF.Exp, scale=float(lg)), reads=[WTAB.b], writes=[DTAB.b])
            wgr = load_unit([(0, 256, wcols("w_in", l, O_GR + hd * 256, 256))], C, 256)
            if hd + 1 < 4:
                aw[hd + 1] = a_weights(hd + 1)
            for ib in range(T):
                njt = 4 * ib + 4
                s_banks = {}

                def issue_s(jt):
                    ps = bank()
                    P.mm_group([lambda pe, ps=ps: pe.matmul(ps.ap, lhsT=KR[:, jt * 128:(jt + 1) * 128], rhs=QR[:, tsl(ib)], start=True, stop=True)],
                               reads=[KRb[jt // 4], QRb[ib]], writes=[ps.b])
                    s_banks[jt] = ps

                issue_s(0)
                issue_s(1)
                for jt in range(njt):
                    if jt + 2 < njt:
                        issue_s(jt + 2)
                    ps = s_banks.pop(jt)
                    pt = ptbuf()
                    r = jt - 4 * ib
                    if r >= 0:
                        tab = DTAB.ap[:, 384 - 128 * r:384 - 128 * r + TT]
                        sc = DK ** -0.5
                    else:
                        tab = DTAB.ap[:, 896:1408]
                        sc = DK ** -0.5 * gamma ** (512 * ib - 128 * jt)
                    P.op("dve", lambda e, ps=ps, pt=pt, tab=tab, sc=sc: e.scalar_tensor_tensor(
                        out=pt.ap, in0=ps.ap, scalar=float(sc), in1=tab, op0=ALU.mult, op1=ALU.mult),
                        reads=[ps.b, DTAB.b], writes=[pt.b])
                    P.mm_group([
                        (lambda pe, e_=e_, pt=pt: pe.matmul(OB[e_].ap, lhsT=VH[:, jt, e_ * 128:(e_ + 1) * 128], rhs=pt.ap,
                                                           start=(jt == 0), stop=(jt == njt - 1)))
                        for e_ in range(2)], reads=[pt.b, VHb[jt // 2]], writes=[OB[0].b, OB[1].b])
                if pend["f"] is not None:
                    pend["f"]()
                    pend["f"] = None
                osb = [stbuf(), stbuf()]
                for e_ in range(2):
                    P.op("act", lambda e, e_=e_: e.activation(out=osb[e_].ap, in_=OB[e_].ap, func=AF.Copy), reads=[OB[e_].b], writes=[osb[e_].b])
                for e_ in range(2):
                    P.op("act", lambda e, e_=e_: e.activation(out=HNB[e_].ap, in_=osb[e_].ap, func=AF.Copy), reads=[osb[e_].b], writes=[HNB[e_].b])
                    P.op("act", lambda e, e_=e_: e.activation(out=HNB[2 + e_].ap, in_=osb[e_].ap, func=AF.Square), reads=[osb[e_].b], writes=[HNB[2 + e_].b])
                if hd + 1 < 4:
                    nst = sets[(hd + 1) % 2]
                    a_proj_tile(nst, ib, aw[hd + 1][0], aw[hd + 1][1])
                    a_v(nst, 2 * ib, aw[hd + 1][2])
                    a_v(nst, 2 * ib + 1, aw[hd + 1][2])
                ps1 = bank()
                ps2 = bank()
                for i in range(2):
                    P.mm_group([lambda pe, i=i: pe.matmul(ps1.ap, lhsT=ONESB.ap, rhs=HNB[i].ap, start=(i == 0), stop=(i == 1))],
                               reads=[HNB[i].b, ONESB.b], writes=[ps1.b])
                for i in range(2):
                    P.mm_group([lambda pe, i=i: pe.matmul(ps2.ap, lhsT=ONESB.ap, rhs=HNB[2 + i].ap, start=(i == 0), stop=(i == 1))],
                               reads=[HNB[2 + i].b, ONESB.b], writes=[ps2.b])
                mean = stbuf()
                P.op("dve", lambda e: e.tensor_scalar(out=mean.ap, in0=ps1.ap, scalar1=1.0 / 256, scalar2=None, op0=ALU.mult),
                     reads=[ps1.b], writes=[mean.b])
                msq = acbuf()
                P.op("dve", lambda e: e.tensor_tensor(out=msq.ap, in0=mean.ap, in1=mean.ap, op=ALU.mult), reads=[mean.b], writes=[msq.b])
                var = acbuf()
                P.op("dve", lambda e: e.scalar_tensor_tensor(out=var.ap, in0=ps2.ap, scalar=1.0 / 256, in1=msq.ap, op0=ALU.mult, op1=ALU.subtract),
                     reads=[ps2.b, msq.b], writes=[var.b])
                P.op("dve", lambda e: e.tensor_scalar(out=var.ap, in0=var.ap, scalar1=0.0, scalar2=None, op0=ALU.max), reads=[var.b], writes=[var.b])
                rstd = stbuf()
                P.op("act", lambda e: e.activation(out=rstd.ap, in_=var.ap, func=AF.Ln, bias=1e-5, scale=1.0), reads=[var.b], writes=[rstd.b])
                P.op("act", lambda e: e.activation(out=rstd.ap, in_=rstd.ap, func=AF.Exp, scale=-0.5), reads=[rstd.b], writes=[rstd.b])
                sgs = []
                for e_ in range(2):
                    pgr = bank()
                    proj(wgr, e_ * 128, ib, pgr)
                    sg = acbuf()
                    P.op("act", lambda e, sg=sg, pgr=pgr: e.activation(out=sg.ap, in_=pgr.ap, func=AF.Silu), reads=[pgr.b], writes=[sg.b])
                    sgs.append(sg)

                def hn_final(hd=hd, ib=ib, osb=osb, mean=mean, rstd=rstd, sgs=sgs):
                    for e_ in range(2):
                        P.op("dve", lambda e, e_=e_: e.tensor_tensor(out=osb[e_].ap, in0=osb[e_].ap, in1=mean.ap, op=ALU.subtract),
                             reads=[osb[e_].b, mean.b], writes=[osb[e_].b])
                        P.op("dve", lambda e, e_=e_: e.tensor_tensor(out=osb[e_].ap, in0=osb[e_].ap, in1=rstd.ap, op=ALU.mult),
                             reads=[osb[e_].b, rstd.b], writes=[osb[e_].b])
                        P.op("dve", lambda e, e_=e_: e.tensor_tensor(out=U[:, 2 * hd + e_, tsl(ib)], in0=osb[e_].ap, in1=sgs[e_].ap, op=ALU.mult),
                             reads=[osb[e_].b, sgs[e_].b], writes=[Ub[2 * hd + e_][ib]])

                pend["f"] = hn_final
        if pend["f"] is not None:
            pend["f"]()
            pend["f"] = None
        rot["banks"] = list(range(8))
        out_branch(l, "w_ret_o", 0, True)

        PB = R2[:, 0:2 + S]
        PBb = [Buf() for _ in range(T)]
        carve(PBb)
        P.op("dve", lambda e: e.memset(PB[:, 0:2], 0.0), writes=[PBb[0]])
        for c in range(C):
            wbc = load_unit([(0, 128, wcols("w_in", l, O_SCB + c * 128, 128)),
                             (128, 128, wcols("w_in", l, O_SCC + c * 128, 128))], C, 256)
            wx = load_unit([(0, 128, wcols("w_in", l, O_SCX + c * 128, 128))], C, 256)
            for t in range(T):
                pc = bank()
                proj(wbc, 128, t, pc)
                px = bank()
                proj(wx, 0, t, px)
                pbk = bank()
                proj(wbc, 0, t, pbk)
                cs = acbuf()
                P.op("act", lambda e, cs=cs, pc=pc: e.activation(out=cs.ap, in_=pc.ap, func=AF.Copy), reads=[pc.b], writes=[cs.b])
                P.op("dve", lambda e, cs=cs, px=px: e.tensor_tensor(out=PB[:, 2 + t * TT:2 + (t + 1) * TT], in0=px.ap, in1=cs.ap, op=ALU.mult),
                     reads=[px.b, cs.b], writes=[PBb[t]])
                cv = acbuf()
                rd = [PBb[t]] + ([PBb[t - 1]] if t > 0 else [])
                wcol = lambda k: pcol(P_SCW + (l * 3 + k) * C + c)
                P.op("dve", lambda e, cv=cv: e.tensor_scalar(out=cv.ap, in0=PB[:, 2 + t * TT:2 + (t + 1) * TT], scalar1=wcol(2), scalar2=None, op0=ALU.mult),
                     reads=rd + [PAR.b], writes=[cv.b])
                P.op("dve", lambda e, cv=cv: e.scalar_tensor_tensor(out=cv.ap, in0=PB[:, 1 + t * TT:1 + (t + 1) * TT], scalar=wcol(1), in1=cv.ap,
                                                                    op0=ALU.mult, op1=ALU.add), reads=rd + [cv.b], writes=[cv.b])
                P.op("dve", lambda e, cv=cv: e.scalar_tensor_tensor(out=cv.ap, in0=PB[:, t * TT:(t + 1) * TT], scalar=wcol(0), in1=cv.ap,
                                                                    op0=ALU.mult, op1=ALU.add), reads=rd + [cv.b], writes=[cv.b])
                P.op("dve", lambda e, cv=cv, pbk=pbk: e.tensor_tensor(out=U[:, c, tsl(t)], in0=pbk.ap, in1=cv.ap, op=ALU.mult),
                     reads=[pbk.b, cv.b], writes=[Ub[c][t]])
        out_branch(l, "w_sc_o", 1, False)

        UG = R2b16[:, 0:30 + S]
        UGb = [Buf() for _ in range(T)]
        DGs = [R2b16[:, 4096 + i * 4096:4096 + i * 4096 + 31 * 128].rearrange("p (k m) -> p k m", k=31) for i in range(2)]
        DGbs = [Buf(), Buf()]
        carve(UGb + DGbs)
        P.op("dve", lambda e: e.memset(UG[:, 0:30], 0.0), writes=[UGb[0]])

        def c_load(c):
            wab = load_unit([(0, 128, wcols("w_in", l, O_GA + c * 128, 128)),
                             (128, 128, wcols("w_in", l, O_GB + c * 128, 128))], C, 256)
            DG, DGb = DGs[c % 2], DGbs[c % 2]
            for k in range(31):
                P.op("dve", lambda e, k=k: e.tensor_scalar(out=DG[:, k, :], in0=IDB.ap, scalar1=pcol(P_CFW + (l * 31 + k) * C + c), scalar2=None, op0=ALU.mult),
                     reads=[IDB.b, PAR.b], writes=[DGb])
            return wab

        def c_proj(c, t, wab):
            pa = bank()
            proj(wab, 0, t, pa)
            pb_ = bank()
            proj(wab, 128, t, pb_)
            sg = acbuf()
            P.op("act", lambda e, sg=sg, pb_=pb_: e.activation(out=sg.ap, in_=pb_.ap, func=AF.Sigmoid), reads=[pb_.b], writes=[sg.b])
            P.op("dve", lambda e, sg=sg, pa=pa: e.tensor_tensor(out=UG[:, 30 + t * TT:30 + (t + 1) * TT], in0=pa.ap, in1=sg.ap, op=ALU.mult),
                 reads=[pa.b, sg.b], writes=[UGb[t]])

        def c_conv(c, t):
            DG, DGb = DGs[c % 2], DGbs[c % 2]
            pcv = bank()
            rd = [UGb[t]] + ([UGb[t - 1]] if t > 0 else [])
            P.mm_group([
                (lambda pe, k=k: pe.matmul(pcv.ap, lhsT=DG[:, k, :], rhs=UG[:, t * TT + k:t * TT + k + TT], start=(k == 0), stop=(k == 30)))
                for k in range(31)], reads=rd + [DGb], writes=[pcv.b])
            P.op("act", lambda e, pcv=pcv: e.activation(out=U[:, c, tsl(t)], in_=pcv.ap, func=AF.Identity, bias=pcol(P_CFB + l * C + c)),
                 reads=[pcv.b, PAR.b], writes=[Ub[c][t]])

        def ln_tile(t):
            mean, rstd = colstats([(U[:, c, tsl(t)], Ub[c][t], U[:, c, tsl(t)]) for c in range(C)], D, 1e-5)
            for c in range(C):
                tmp = acbuf()
                P.op("dve", lambda e, tmp=tmp: e.tensor_tensor(out=tmp.ap, in0=U[:, c, tsl(t)], in1=mean.ap, op=ALU.subtract),
                     reads=[Ub[c][t], mean.b], writes=[tmp.b])
                P.op("dve", lambda e, tmp=tmp: e.scalar_tensor_tensor(out=tmp.ap, in0=tmp.ap, scalar=pcol(P_LNG + l * C + c), in1=rstd.ap,
                                                                      op0=ALU.mult, op1=ALU.mult), reads=[tmp.b, rstd.b, PAR.b], writes=[tmp.b])
                P.op("act", lambda e, tmp=tmp: e.activation(out=U[:, c, tsl(t)], in_=tmp.ap, func=AF.Silu, bias=pcol(P_LNB + l * C + c)),
                     reads=[tmp.b, PAR.b], writes=[Ub[c][t]])

        csteps = [(c, t) for c in range(C) for t in range(T)]
        pending = []
        cw = {0: c_load(0)}
        c_proj(0, 0, cw[0])
        c_proj(0, 1, cw[0])
        for i, (c, t) in enumerate(csteps):
            c_conv(c, t)
            if c == C - 1:
                ln_tile(t)
            j = i + 2
            if j < len(csteps):
                c2, t2 = csteps[j]
                if c2 == c:
                    c_proj(c2, t2, cw[c2])
                else:
                    if c2 not in cw:
                        cw[c2] = c_load(c2)
                    pending.append((c2, t2))
            still = []
            for (c2, t2) in pending:
                need = (c2 - 1, min(t2 + 1, T - 1))
                if csteps.index(need) <= i:
                    c_proj(c2, t2, cw[c2])
                else:
                    still.append((c2, t2))
            pending[:] = still
        out_branch(l, "w_cf_o", 2, False)

        WO = TB(R2b16[:, 0:8192].rearrange("p (k n) -> p k n", k=C))
        XS2 = XSt(R2[:, 4096:8192].rearrange("p (c n) -> p c n", c=C))
        carve([WO.b] + XS2.bs)
        for q4 in range(4):
            P.dma("pool", WO.ap[:, :, q4 * 256:(q4 + 1) * 256], wcols("w_o", l, q4 * 256, 256), writes=[WO.b])

        def mt(i, c):
            cc, tt = 4 + 2 * i + c // 4, c % 4
            return ACC[:, cc, tsl(tt)], ACCb[cc][tt]

        for t in range(T):
            for cp in range(C):
                pm = bank()
                proj(WO, cp * 128, t, pm, src=M, srcb=Mb)
                ap, b_ = mt(t % 2, cp)
                if cp % 2 == 0:
                    P.op("act", lambda e, pm=pm, ap=ap: e.activation(out=ap, in_=pm.ap, func=AF.Copy), reads=[pm.b], writes=[b_])
                else:
                    P.op("dve", lambda e, pm=pm, ap=ap: e.tensor_copy(out=ap, in_=pm.ap), reads=[pm.b], writes=[b_])
            postnorm_tile(l, seq, 3, t, lambda c: mt(t % 2, c), 1.0, nxt, XST if t % 2 == 0 else XS2)

    XF = TB(R2[:, 0:4096].rearrange("p (c n) -> p c n", c=C))
    XTOK = XST.ap.rearrange("p c n -> p (c n)").rearrange("p (a d) -> p a d", a=4)
    OTb = Buf()

    def init_tile(seq, t):
        carve([XF.b])
        carve(XST.bs, "XST")
        P.dma("sp", XTOK, x_d[seq, tsl(t), :].rearrange("(a p) d -> p a d", p=128), writes=XST.bs)
        for c in range(C):
            pb = bank()
            P.mm_group([
                (lambda pe, a=a: pe.transpose(out=pb.ap[:, a * 128:(a + 1) * 128], in_=XTOK[:, a, c * 128:(c + 1) * 128], identity=IDENT))
                for a in range(4)], reads=XST.bs + [PAR.b], writes=[pb.b])
            P.op("act", lambda e, pb=pb: e.activation(out=XF.ap[:, c, :], in_=pb.ap, func=AF.Copy), reads=[pb.b], writes=[XF.b])
        P.dma("sp", xs_d[seq, :, :, tsl(t)].rearrange("c p n -> p c n"), XF.ap, reads=[XF.b], writes=[xsb[seq][t]])

    def init_seq(seq):
        for t in range(T):
            init_tile(seq, t)

    def out_tile(seq, t, src):
        carve([OTb], "XST")
        for a in range(4):
            for half in range(2):
                pb = bank()
                P.mm_group([
                    (lambda pe, c4=c4: pe.transpose(out=pb.ap[:, c4 * 128:(c4 + 1) * 128],
                                                    in_=src.ap[:, half * 4 + c4, a * 128:(a + 1) * 128], identity=IDENT))
                    for c4 in range(4)], reads=[src.bs[half * 4 + c4] for c4 in range(4)] + [PAR.b], writes=[pb.b])
                P.op("act", lambda e, pb=pb: e.activation(out=XTOK[:, a, half * 512:(half + 1) * 512], in_=pb.ap, func=AF.Copy),
                     reads=[pb.b], writes=[OTb])
        return P.dma("sp", out_d[seq, tsl(t), :].rearrange("(a p) d -> p a d", p=128), XTOK, reads=[OTb], writes=[outb[seq][t]])

    def output_seq(seq):
        OT = TB(R2[:, 0:4096].rearrange("p (a d) -> p a d", a=4))
        carve([OT.b])
        toks = []
        for t in range(T):
            load_x(seq, t)
            for a in range(4):
                for half in range(2):
                    pb = bank()
                    P.mm_group([
                        (lambda pe, c4=c4: pe.transpose(out=pb.ap[:, c4 * 128:(c4 + 1) * 128],
                                                        in_=XST.ap[:, half * 4 + c4, a * 128:(a + 1) * 128], identity=IDENT))
                        for c4 in range(4)], reads=XST.bs + [PAR.b], writes=[pb.b])
                    if half == 0:
                        P.op("act", lambda e, pb=pb: e.activation(out=OT.ap[:, a, 0:512], in_=pb.ap, func=AF.Copy), reads=[pb.b], writes=[OT.b])
                    else:
                        P.op("dve", lambda e, pb=pb: e.tensor_copy(out=OT.ap[:, a, 512:1024], in_=pb.ap), reads=[pb.b], writes=[OT.b])
            toks.append(P.dma("sp", out_d[seq, tsl(t), :].rearrange("(a p) d -> p a d", p=128), OT.ap, reads=[OT.b], writes=[outb[seq][t]]))
        return toks

    PRE_IDX = {0: 0, 1: 2, 2: 4}
    subs = [(l, sub) for l in range(nl) for sub in range(3)]
    if stop is not None:
        subs = subs[:stop]
    fused_io = len(subs) > 0 and subs[-1][1] == 2
    for seq in range(nseq):
        if seq == 0 or not fused_io:
            init_seq(seq)
        for i, (l, sub) in enumerate(subs):
            nxt = (subs[i + 1][0], PRE_IDX[subs[i + 1][1]]) if i + 1 < len(subs) else None
            do_pre = (i == 0) and (seq == 0 or not fused_io)
            last = (i == len(subs) - 1)
            if sub == 0:
                hooks = None
                rope_needed["v"] = True
                if nxt is not None:
                    hooks = {12 + 8 * t: (lambda t=t: rope_tile(seq, t)) for t in range(T)}
                    rope_needed["v"] = False
                ffn(l, seq, "ffn1_w_gu", "ffn1_w_down", 0, 1, do_pre, nxt, hooks)
            elif sub == 1:
                mixer(l, seq, do_pre, nxt)
                rope_needed["v"] = True
            else:
                hooks = None
                nseq_next = None
                if last and fused_io and seq + 1 < nseq:
                    hooks = {14 + 10 * t: (lambda t=t: init_tile(seq + 1, t)) for t in range(T)}
                    nseq_next = seq + 1
                ffn(l, seq, "ffn2_w_gu", "ffn2_w_down", 4, 5, do_pre, nxt, hooks, final=(last and fused_io), next_seq=nseq_next)
        if not fused_io:
            all_out += output_seq(seq)
    for t in all_out:
        P._wait("sp", t)
    return nc


def pack_consts(norm_g, sc_conv_w, cf_dw_w, cf_dw_b, cf_ln_g, cf_ln_b):
    par = np.zeros((128, NPAR), np.float32)

    def fm(a):
        a = np.asarray(a, np.float32)
        lead = a.shape[:-1]
        a = a.reshape(lead + (C, 128))
        a = np.moveaxis(a, -1, 0)
        return a.reshape(128, -1)

    par[:, P_NG:P_NG + L * 6 * C] = fm(norm_g)
    par[:, P_SCW:P_SCW + L * 3 * C] = fm(sc_conv_w)
    par[:, P_CFW:P_CFW + L * 31 * C] = fm(cf_dw_w)
    par[:, P_CFB:P_CFB + L * C] = fm(cf_dw_b)
    par[:, P_LNG:P_LNG + L * C] = fm(cf_ln_g)
    par[:, P_LNB:P_LNB + L * C] = fm(cf_ln_b)
    invf = (10000.0 ** (-np.arange(64, dtype=np.float32) / 64.0)).astype(np.float32)
    par[:, P_INVF] = np.concatenate([invf, invf])
    par[:64, P_SGN] = -1.0
    par[64:, P_SGN] = 1.0
    par[:, P_ID:P_ID + 128] = np.eye(128, dtype=np.float32)
    j = np.arange(128)[:, None]
    x = np.arange(-384, 512)[None, :]
    ok = (x >= 0) & ((j // 64) <= (np.floor_divide(x, 64)))
    wn = np.where(ok, np.abs(x - j), BIGD).astype(np.float32)
    xf = np.arange(512)[None, :]
    wf = (xf - j).astype(np.float32)
    wtab = np.concatenate([wn, wf], axis=1).astype(np.float32)
    return par, wtab


_NC_CACHE = {}


def kernel(x, positions, norm_g, ffn1_w_gu, ffn1_w_down, w_in, w_ret_o, sc_conv_w, w_sc_o,
           cf_dw_w, cf_dw_b, cf_ln_g, cf_ln_b, w_cf_o, w_o, ffn2_w_gu, ffn2_w_down):
    x = np.ascontiguousarray(np.asarray(x, np.float32))
    positions = np.ascontiguousarray(np.asarray(positions, np.int32))
    par, wtab = pack_consts(np.asarray(norm_g), np.asarray(sc_conv_w), np.asarray(cf_dw_w), np.asarray(cf_dw_b),
                            np.asarray(cf_ln_g), np.asarray(cf_ln_b))
    ws = {"ffn1_w_gu": ffn1_w_gu, "ffn1_w_down": ffn1_w_down, "w_in": w_in, "w_ret_o": w_ret_o, "w_sc_o": w_sc_o,
          "w_cf_o": w_cf_o, "w_o": w_o, "ffn2_w_gu": ffn2_w_gu, "ffn2_w_down": ffn2_w_down}
    ws = {k: np.ascontiguousarray(np.asarray(v, np.float32)) for k, v in ws.items()}
    if "nc" not in _NC_CACHE:
        _NC_CACHE["nc"] = build()
    nc = _NC_CACHE["nc"]
    in_maps = []
    for i in range(NCORES):
        m = {"x": x[NSEQ * i:NSEQ * (i + 1)], "pos": positions[NSEQ * i:NSEQ * (i + 1)], "par": par, "wtab": wtab}
        m.update(ws)
        in_maps.append(m)
    res = run_bass_kernel_spmd(nc, in_maps, core_ids=list(range(NCORES)))
    return np.concatenate([np.asarray(r["out"]) for r in res.results], axis=0).astype(np.float32)
```
